# Optimizing a Trainium2 kernel written in Bass

```python
import jax, jax.numpy as jnp
from jax import lax
import numpy as np

D_MODEL = 2048
BATCH = 1
SEQ = 8192
DEPTH = 2

GRID_W = 64
HEAD_DIM = 128
D_MIX = D_MODEL
N_HEADS_NA = (D_MIX // 2) // HEAD_DIM
N_HEADS_Q = (D_MIX // 2) // HEAD_DIM
N_KV_HEADS = 2
D_NA = N_HEADS_NA * HEAD_DIM
D_GQA = N_HEADS_Q * HEAD_DIM
D_KV = N_KV_HEADS * HEAD_DIM
D_IN = 3 * D_NA + D_GQA + 2 * D_KV
NA_KH = 8
NA_KW = 16
Q_BLOCK = 128
ROPE_THETA = 10000.0
D_FF = 5632
RMS_EPS = 1e-6
LN_EPS = 1e-5
DEEPNORM_ALPHA = (2.0 * DEPTH) ** 0.25
DEEPNORM_BETA = (8.0 * DEPTH) ** -0.25

kernel_name = "hybrid_natten_gqa_macaron_deepnorm"


def layer_norm(x, g, b):
    xf = x.astype(jnp.float32)
    mu = jnp.mean(xf, axis=-1, keepdims=True)
    xc = xf - mu
    var = jnp.mean(xc * xc, axis=-1, keepdims=True)
    y = xc * lax.rsqrt(var + LN_EPS)
    return (y * g.astype(jnp.float32) + b.astype(jnp.float32)).astype(x.dtype)


def rms_norm(x, g):
    xf = x.astype(jnp.float32)
    y = xf * lax.rsqrt(jnp.mean(xf * xf, axis=-1, keepdims=True) + RMS_EPS)
    return (y * g.astype(jnp.float32)).astype(x.dtype)


def swiglu(x, w_gate_up, w_down):
    gate, up = jnp.split(x @ w_gate_up, 2, axis=-1)
    return (jax.nn.silu(gate) * up) @ w_down


def axial_rope(x):
    _, S, _, D = x.shape
    half = D // 2
    nfreq = half // 2
    t = jnp.arange(S)
    row = (t // GRID_W).astype(jnp.float32)
    col = (t % GRID_W).astype(jnp.float32)
    inv_freq = 1.0 / (ROPE_THETA ** (jnp.arange(nfreq, dtype=jnp.float32) / nfreq))

    def rot(xh, pos):
        ang = pos[:, None] * inv_freq[None, :]
        cos = jnp.cos(ang)[None, :, None, :]
        sin = jnp.sin(ang)[None, :, None, :]
        x1, x2 = xh[..., :nfreq], xh[..., nfreq:]
        return jnp.concatenate([x1 * cos - x2 * sin, x2 * cos + x1 * sin], axis=-1)

    xf = x.astype(jnp.float32)
    out = jnp.concatenate([rot(xf[..., :half], row), rot(xf[..., half:], col)], axis=-1)
    return out.astype(x.dtype)


def neighbourhood_attention(q, k, v, rel_bias):
    B, S, H, D = q.shape
    rows = S // GRID_W
    kh = min(NA_KH, rows)
    r = jnp.arange(rows)
    r0 = jnp.clip(r - kh // 2, 0, rows - kh)
    row_idx = r0[:, None] + jnp.arange(kh)[None, :]
    c = jnp.arange(GRID_W)
    c0 = jnp.clip(c - NA_KW // 2, 0, GRID_W - NA_KW)
    kc = jnp.arange(GRID_W)
    col_in = (kc[None, :] >= c0[:, None]) & (kc[None, :] < c0[:, None] + NA_KW)
    drow_idx = row_idx - r[:, None] + (NA_KH - 1)
    dcol_idx = jnp.clip(kc[None, :] - c[:, None], -(NA_KW - 1), NA_KW - 1) + (NA_KW - 1)
    bias = rel_bias[:, drow_idx[:, :, None, None], dcol_idx[None, None, :, :]]
    bias = jnp.transpose(bias, (0, 1, 3, 2, 4)).astype(jnp.float32)

    scale = HEAD_DIM ** -0.5
    qg = q.reshape(B, rows, GRID_W, H, D)
    kg = k.reshape(B, rows, GRID_W, H, D)[:, row_idx]
    vg = v.reshape(B, rows, GRID_W, H, D)[:, row_idx]
    s = jnp.einsum('brchd,brukhd->bhrcuk', qg, kg).astype(jnp.float32) * scale
    s = s + bias[None]
    s = jnp.where(col_in[None, None, None, :, None, :], s, -jnp.inf)
    sh = s.shape
    p = jax.nn.softmax(s.reshape(sh[:4] + (kh * GRID_W,)), axis=-1).reshape(sh)
    o = jnp.einsum('bhrcuk,brukhd->brchd', p.astype(v.dtype), vg)
    return o.reshape(B, S, H, D)


def gqa_block_attention(q, k, v):
    B, S, Hq, D = q.shape
    Hkv = k.shape[2]
    G = Hq // Hkv
    nblk = S // Q_BLOCK
    scale = HEAD_DIM ** -0.5
    qb = jnp.transpose(q.reshape(B, nblk, Q_BLOCK, Hkv, G, D), (1, 0, 2, 3, 4, 5))

    def one_block(qblk):
        s = jnp.einsum('bqhgd,bkhd->bhgqk', qblk, k).astype(jnp.float32) * scale
        p = jax.nn.softmax(s, axis=-1)
        return jnp.einsum('bhgqk,bkhd->bqhgd', p.astype(v.dtype), v)

    o = lax.map(one_block, qb)
    return jnp.transpose(o, (1, 0, 2, 3, 4, 5)).reshape(B, S, Hq * D)


def setup_inputs(seed: int = 0) -> dict:
    key = jax.random.key(seed)
    ks = jax.random.split(key, 20)
    f32 = jnp.float32

    def nrm(k, shape, scale):
        return jax.random.normal(k, shape, f32) * scale

    def gain(k, shape):
        return 1.0 + 0.05 * jax.random.normal(k, shape, f32)

    return {
        "x": jax.random.normal(ks[0], (BATCH, SEQ, D_MODEL), f32),
        "ffn1_w_gate_up": nrm(ks[1], (DEPTH, D_MODEL, 2 * D_FF), D_MODEL ** -0.5),
        "ffn1_w_down": nrm(ks[2], (DEPTH, D_FF, D_MODEL), DEEPNORM_BETA * D_FF ** -0.5),
        "ln1_g": gain(ks[3], (DEPTH, D_MODEL)),
        "ln1_b": nrm(ks[4], (DEPTH, D_MODEL), 0.02),
        "w_in": nrm(ks[5], (DEPTH, D_MODEL, D_IN), D_MODEL ** -0.5),
        "na_rel_bias": nrm(ks[6], (DEPTH, N_HEADS_NA, 2 * NA_KH - 1, 2 * NA_KW - 1), 0.1),
        "q_norm_g": gain(ks[7], (DEPTH, HEAD_DIM)),
        "k_norm_g": gain(ks[8], (DEPTH, HEAD_DIM)),
        "gn_na_g": gain(ks[9], (DEPTH, D_NA)),
        "gn_gqa_g": gain(ks[10], (DEPTH, D_GQA)),
        "w_out": nrm(ks[11], (DEPTH, D_MIX, D_MODEL), DEEPNORM_BETA * D_MIX ** -0.5),
        "ln2_g": gain(ks[12], (DEPTH, D_MODEL)),
        "ln2_b": nrm(ks[13], (DEPTH, D_MODEL), 0.02),
        "ffn2_w_gate_up": nrm(ks[14], (DEPTH, D_MODEL, 2 * D_FF), D_MODEL ** -0.5),
        "ffn2_w_down": nrm(ks[15], (DEPTH, D_FF, D_MODEL), DEEPNORM_BETA * D_FF ** -0.5),
        "ln3_g": gain(ks[16], (DEPTH, D_MODEL)),
        "ln3_b": nrm(ks[17], (DEPTH, D_MODEL), 0.02),
    }


def reference(x, ffn1_w_gate_up, ffn1_w_down, ln1_g, ln1_b, w_in, na_rel_bias,
              q_norm_g, k_norm_g, gn_na_g, gn_gqa_g, w_out, ln2_g, ln2_b,
              ffn2_w_gate_up, ffn2_w_down, ln3_g, ln3_b):
    B, S, _ = x.shape
    for l in range(DEPTH):
        x = layer_norm(DEEPNORM_ALPHA * x + 0.5 * swiglu(x, ffn1_w_gate_up[l], ffn1_w_down[l]),
                       ln1_g[l], ln1_b[l])

        h = x @ w_in[l]
        q_na, k_na, v_na, q_g, k_g, v_g = jnp.split(
            h, np.cumsum([D_NA, D_NA, D_NA, D_GQA, D_KV]).tolist(), axis=-1)

        o_na = neighbourhood_attention(
            q_na.reshape(B, S, N_HEADS_NA, HEAD_DIM),
            k_na.reshape(B, S, N_HEADS_NA, HEAD_DIM),
            v_na.reshape(B, S, N_HEADS_NA, HEAD_DIM),
            na_rel_bias[l]).reshape(B, S, D_NA)

        qh = axial_rope(rms_norm(q_g.reshape(B, S, N_HEADS_Q, HEAD_DIM), q_norm_g[l]))
        kh = axial_rope(rms_norm(k_g.reshape(B, S, N_KV_HEADS, HEAD_DIM), k_norm_g[l]))
        vh = v_g.reshape(B, S, N_KV_HEADS, HEAD_DIM)
        o_g = gqa_block_attention(qh, kh, vh)

        mix = jnp.concatenate([rms_norm(o_na, gn_na_g[l]), rms_norm(o_g, gn_gqa_g[l])], axis=-1) @ w_out[l]
        x = layer_norm(DEEPNORM_ALPHA * x + mix, ln2_g[l], ln2_b[l])

        x = layer_norm(DEEPNORM_ALPHA * x + 0.5 * swiglu(x, ffn2_w_gate_up[l], ffn2_w_down[l]),
                       ln3_g[l], ln3_b[l])
    return x
```

```python
import numpy as np
import ml_dtypes
import concourse.bass as bass
import concourse.mybir as mybir
from concourse.bass_utils import run_bass_kernel_spmd

F32 = mybir.dt.float32
BF16 = mybir.dt.bfloat16
AF = mybir.ActivationFunctionType
ALU = mybir.AluOpType
AX = mybir.AxisListType

NCORES = 8
D = 2048
DFF = 5632
T = 1024
NT = T // 128
NDC = D // 128
NFC = DFF // 128
G = 4
NG = NFC // G
DEPTH = 2
ALPHA = (2.0 * DEPTH) ** 0.25
LN_EPS = 1.0001e-5
RMS_EPS = 1e-6
NDMASEM = 12
USE_CC = False


class Op:
    __slots__ = ("eng", "fns", "deps", "sig", "dma", "dma_idx", "val", "sem", "cc")

    def __init__(self, eng, fns, dma):
        self.eng = eng
        self.fns = fns
        self.dma = dma
        self.deps = ()
        self.sig = False
        self.dma_idx = -1
        self.val = 0
        self.sem = None
        self.cc = False


class Sched:
    ENGS = ("pe", "act", "dve", "pool", "sp")

    def __init__(self):
        self.ops = {e: [] for e in self.ENGS}
        self.last_w = {}
        self.readers = {}
        self.dma_hist = {e: [] for e in self.ENGS}
        self.cc_ops = []

    def op(self, eng, fns, reads=(), writes=(), dma=False, extra_deps=(), cc=False):
        if not isinstance(fns, (list, tuple)):
            fns = [fns]
        o = Op(eng, list(fns), dma)
        deps = set(extra_deps)
        for k in reads:
            w = self.last_w.get(k)
            if w is not None:
                deps.add(w)
        for k in writes:
            w = self.last_w.get(k)
            if w is not None:
                deps.add(w)
            for r in self.readers.get(k, ()):
                deps.add(r)
        for k in reads:
            self.readers.setdefault(k, []).append(o)
        for k in writes:
            self.last_w[k] = o
            self.readers[k] = []
        if cc:
            o.cc = True
            self.cc_ops.append(o)
        if dma:
            hist = self.dma_hist[eng]
            if len(hist) >= NDMASEM:
                deps.add(hist[-NDMASEM])
            o.dma_idx = len(hist)
            hist.append(o)
        deps.discard(o)
        o.deps = [d for d in deps if not (d.eng == "pe" and eng == "pe" and not d.dma)]
        for d in o.deps:
            d.sig = True
        self.ops[eng].append(o)
        return o

    def emit(self, nc, block, sems, dma_sems, cc_sems=()):
        for i, o in enumerate(self.cc_ops):
            o.sem = cc_sems[i]
            o.val = 1
        for e in self.ENGS:
            cnt = 0
            for o in self.ops[e]:
                if o.cc:
                    continue
                if o.dma:
                    o.sem = dma_sems[e][o.dma_idx % NDMASEM]
                    o.val = 16 * (o.dma_idx // NDMASEM + 1)
                elif o.sig:
                    cnt += 1
                    o.sem = sems[e]
                    o.val = cnt

        def run(e, engine):
            seen = {}
            for o in self.ops[e]:
                for d in o.deps:
                    key = id(d.sem)
                    if seen.get(key, 0) < d.val:
                        engine.wait_ge(d.sem, d.val)
                        seen[key] = d.val
                ins = None
                for fn in o.fns:
                    ins = fn(engine)
                if o.cc:
                    ins.then_inc(o.sem)
                elif o.dma:
                    ins.then_inc(o.sem, 16)
                elif o.sig:
                    assert ins is not None
                    ins.then_inc(o.sem, 1)

        @block.tensor
        def _(eng):
            run("pe", eng)

        @block.scalar
        def _(eng):
            run("act", eng)

        @block.vector
        def _(eng):
            run("dve", eng)

        @block.gpsimd
        def _(eng):
            run("pool", eng)

        @block.sync
        def _(eng):
            run("sp", eng)


class Ctx:
    pass


def emit_load_x(C, x_dram):
    S = C.S
    xv = x_dram.rearrange("(n p) d -> p n d", p=128)
    for t in range(NT):
        S.op("sp", lambda e, t=t: e.dma_start(out=C.xres[:, t, :], in_=xv[:, t, :]),
             writes=[f"xres{t}_{b}" for b in range(4)], dma=True)


def emit_make_xT(C):
    S = C.S
    for t in range(NT):
        S.op("act", lambda e, t=t: e.activation(out=C.xbf[:, :], in_=C.xres[:, t, :], func=AF.Copy),
             reads=[f"xres{t}_{b}" for b in range(4)], writes=["xbf"])
        for hh in range(2):
            bank = C.tp_bank(hh)
            pst = C.psT[:, bank, :]
            fns = []
            for k in range(8):
                c = hh * 8 + k
                fns.append(lambda e, c=c, k=k, pst=pst: e.transpose(
                    out=pst[:, k * 128:(k + 1) * 128], in_=C.xbf[:, c * 128:(c + 1) * 128], identity=C.ident[:, :]))
            S.op("pe", fns, reads=["xbf", "ident"], writes=[f"ps{bank}"])
            S.op("dve", lambda e, t=t, hh=hh, pst=pst: e.tensor_copy(
                out=C.xT[:, hh * 8:(hh + 1) * 8, t * 128:(t + 1) * 128],
                in_=pst.rearrange("p (k j) -> p k j", k=8)),
                reads=[f"ps{bank}"], writes=[f"xT{t}"])


def emit_ffn_ln(C, wgu_dram, wd_dram, g_dram, b_dram, NG=NG, wname="w"):
    S = C.S
    S.op("sp", lambda e: e.dma_start(out=C.lng[:, :], in_=g_dram.partition_broadcast(128)), writes=["lng"], dma=True)
    S.op("sp", lambda e: e.dma_start(out=C.lnb[:, :], in_=b_dram.partition_broadcast(128)), writes=["lnb"], dma=True)

    for t in range(NT):
        S.op("pool", lambda e, t=t: e.tensor_scalar(out=C.xres[:, t, :], in0=C.xres[:, t, :], scalar1=float(ALPHA),
                                                     scalar2=None, op0=ALU.mult),
             reads=[f"xres{t}_{b}" for b in range(4)], writes=[f"xres{t}_{b}" for b in range(4)])

    def load_wgu(i):
        slot = C.wgu_ctr % C.NWGU
        C.wgu_ctr += 1
        j, r0 = i // 11, (i % 11) * 128
        S.op("pool", lambda e: e.dma_start(out=C.wgu[:, slot, :], in_=wgu_dram[j][r0:r0 + 128, :]),
             reads=[f"{wname}gu_g{j}"], writes=[f"wgu{slot}"], dma=True)
        return slot

    def load_wd(g):
        slot = C.wd_ctr % 2
        C.wd_ctr += 1
        S.op("pool", lambda e: e.dma_start(out=C.wd[:, slot, :], in_=wd_dram[0][g * 128:(g + 1) * 128, :]),
             reads=[f"{wname}d_g0"], writes=[f"wd{slot}"], dma=True)
        return slot

    unit_ctr = [0]

    def up_group(g, wslots):
        gs = g % 2
        for q in range(G):
            ws = wslots[q]
            for h in range(2):
                u = unit_ctr[0]
                unit_ctr[0] += 1
                bg, bu = (0, 1) if u % 2 == 0 else (2, 3)
                xkeys = [f"xT{t}" for t in range(h * 4, h * 4 + 4)]
                for (bank, s) in ((bg, 0), (bu, 1)):
                    fns = []
                    for c in range(NDC):
                        fns.append(lambda e, c=c, bank=bank, s=s, ws=ws, h=h: e.matmul(
                            C.ps[:, bank, :], lhsT=C.wgu[:, ws, (c * 2 + s) * 128:(c * 2 + s + 1) * 128],
                            rhs=C.xT[:, c, h * 512:(h + 1) * 512], start=(c == 0), stop=(c == NDC - 1)))
                    S.op("pe", fns, reads=[f"wgu{ws}"] + xkeys, writes=[f"ps{bank}"])
                sl = u % 2
                S.op("act", lambda e, bg=bg, sl=sl: e.activation(out=C.silu[:, sl, :], in_=C.ps[:, bg, :], func=AF.Silu),
                     reads=[f"ps{bg}"], writes=[f"silu{sl}"])
                S.op("dve", lambda e, bu=bu, sl=sl, gs=gs, q=q, h=h: e.tensor_tensor(
                    out=C.gT[:, gs, q, h * 512:(h + 1) * 512], in0=C.ps[:, bu, :], in1=C.silu[:, sl, :], op=ALU.mult),
                    reads=[f"ps{bu}", f"silu{sl}"], writes=[f"gT{gs}_{q}_{h}"])

    dctr = [0]

    def down_group(g, wdslot):
        gs = g % 2
        for t in range(NT):
            h = t // 4
            for b in range(4):
                bank = 4 + dctr[0] % 4
                dctr[0] += 1
                fns = []
                for q in range(G):
                    fns.append(lambda e, q=q, bank=bank, t=t, b=b: e.matmul(
                        C.ps[:, bank, :], lhsT=C.gT[:, gs, q, t * 128:(t + 1) * 128],
                        rhs=C.wd[:, wdslot, q * D + b * 512:q * D + (b + 1) * 512], start=(q == 0), stop=(q == G - 1)))
                S.op("pe", fns, reads=[f"wd{wdslot}"] + [f"gT{gs}_{q}_{h}" for q in range(G)], writes=[f"ps{bank}"])
                S.op("dve", lambda e, bank=bank, t=t, b=b: e.scalar_tensor_tensor(
                    out=C.xres[:, t, b * 512:(b + 1) * 512], in0=C.ps[:, bank, :], scalar=0.5,
                    in1=C.xres[:, t, b * 512:(b + 1) * 512], op0=ALU.mult, op1=ALU.add),
                    reads=[f"ps{bank}", f"xres{t}_{b}"], writes=[f"xres{t}_{b}"])

    wsl = {}
    wdl = {}
    wsl[0] = [load_wgu(q) for q in range(G)]
    wdl[0] = load_wd(0)
    up_group(0, wsl[0])
    for g in range(NG):
        if g + 1 < NG:
            wsl[g + 1] = [load_wgu((g + 1) * G + q) for q in range(G)]
            wdl[g + 1] = load_wd(g + 1)
            up_group(g + 1, wsl[g + 1])
        down_group(g, wdl[g])

    emit_ln(C)


def emit_ln(C):
    S = C.S
    for t in range(NT):
        for b in range(4):
            S.op("dve", lambda e, t=t, b=b: e.bn_stats(out=C.bnst[:, t, b * 6:(b + 1) * 6], in_=C.xres[:, t, b * 512:(b + 1) * 512]),
                 reads=[f"xres{t}_{b}"], writes=[f"bnst{t}_{b}"])
    for t in range(NT):
        S.op("dve", lambda e, t=t: e.bn_aggr(out=C.mv[:, t, :], in_=C.bnst[:, t, :]),
             reads=[f"bnst{t}_{b}" for b in range(4)], writes=[f"mv{t}"])
    mvk = [f"mv{t}" for t in range(NT)]
    S.op("act", lambda e: e.activation(out=C.rstd[:, :], in_=C.mv[:, :, 1], func=AF.Ln, bias=float(LN_EPS), scale=1.0),
         reads=mvk, writes=["rstd"])
    S.op("act", lambda e: e.activation(out=C.rstd[:, :], in_=C.rstd[:, :], func=AF.Exp, scale=-0.5),
         reads=["rstd"], writes=["rstd"])
    S.op("dve", lambda e: e.scalar_tensor_tensor(out=C.nmr[:, :], in0=C.mv[:, :, 0], scalar=-1.0, in1=C.rstd[:, :],
                                                 op0=ALU.mult, op1=ALU.mult), reads=mvk + ["rstd"], writes=["nmr"])
    for t in range(NT):
        xk = [f"xres{t}_{b}" for b in range(4)]
        S.op("dve", lambda e, t=t: e.tensor_scalar(out=C.xres[:, t, :], in0=C.xres[:, t, :], scalar1=C.mv[:, t, 0:1],
                                                   scalar2=C.rstd[:, t:t + 1], op0=ALU.subtract, op1=ALU.mult),
             reads=xk + mvk + ["rstd"], writes=xk)
        S.op("dve", lambda e, t=t: e.tensor_tensor(out=C.xres[:, t, :], in0=C.xres[:, t, :], in1=C.lng[:, :], op=ALU.mult),
             reads=xk + ["lng"], writes=xk)
        S.op("dve", lambda e, t=t: e.tensor_tensor(out=C.xres[:, t, :], in0=C.xres[:, t, :], in1=C.lnb[:, :], op=ALU.add),
             reads=xk + ["lnb"], writes=xk)


def emit_store_x(C, y_dram):
    S = C.S
    yv = y_dram.rearrange("(n p) d -> p n d", p=128)
    outs = []
    for t in range(NT):
        outs.append(S.op("sp", lambda e, t=t: e.dma_start(out=yv[:, t, :], in_=C.xres[:, t, :]),
                         reads=[f"xres{t}_{b}" for b in range(4)], dma=True))
    S.op("sp", [], extra_deps=outs)


class Builder:
    def __init__(self):
        self.nc = bass.Bass("TRN2", target_bir_lowering=False)
        self.C = Ctx()
        self.C.S = Sched()
        self.C.nc = self.nc

    def dram_in(self, name, shape, dt=F32):
        return self.nc.dram_tensor(name, list(shape), dt, kind="ExternalInput").ap()

    def gathered(self, name, nchunks, rows_per_chunk, cols, dt=F32):
        nc, S = self.nc, self.C.S
        if not USE_CC:
            self.last_cc = None
            return [nc.dram_tensor(f"{name}{j}", [rows_per_chunk, cols], dt, kind="ExternalInput").ap() for j in range(nchunks)]
        rpc = rows_per_chunk // NCORES
        assert rpc * NCORES == rows_per_chunk
        fulls = []
        for j in range(nchunks):
            ext = nc.dram_tensor(f"{name}{j}", [rpc, cols], dt, kind="ExternalInput")
            bnc = nc.dram_tensor(f"{name}{j}_bnc", [rpc, cols], dt)
            full = nc.dram_tensor(f"{name}{j}_full", [rows_per_chunk, cols], dt)
            prev = [self.last_cc] if getattr(self, "last_cc", None) is not None else []
            S.op("pool", lambda e, bnc=bnc, ext=ext: e.dma_start(out=bnc[:, :], in_=ext[:, :]), writes=[f"{name}_b{j}"],
                 dma=True, extra_deps=prev)
            self.last_cc = S.op("pool", lambda e, bnc=bnc, full=full: e.collective_compute(
                "AllGather", ALU.bypass, replica_groups=[list(range(NCORES))],
                ins=[bnc.ap().opt()], outs=[full.ap().opt()]),
                reads=[f"{name}_b{j}"], writes=[f"{name}_g{j}"], cc=True)
            fulls.append(full.ap())
        return fulls

    def dram_out(self, name, shape, dt=F32):
        return self.nc.dram_tensor(name, list(shape), dt, kind="ExternalOutput").ap()


def build_ffn_prog(NG=NG):
    B = Builder()
    nc, C = B.nc, B.C
    x = B.dram_in("x", [T, D])
    wgu = B.gathered("wgu", 4, 11 * 128, NDC * 2 * 128)
    wd = B.gathered("wd", 1, 11 * 128, G * D)
    lg = B.dram_in("ln_g", [1, D])
    lb = B.dram_in("ln_b", [1, D])
    ident = B.dram_in("ident", [128, 128])
    y = B.dram_out("y", [T, D])
    C.NWGU = 4
    C.wgu_ctr = 0
    C.wd_ctr = 0
    with (
        nc.sbuf_tensor("xres", [128, NT, D], F32) as xres,
        nc.sbuf_tensor("xT", [128, NDC, T], BF16) as xT,
        nc.sbuf_tensor("xbf", [128, D], BF16) as xbf,
        nc.sbuf_tensor("identb", [128, 128], BF16) as identb,
        nc.sbuf_tensor("wgu_sb", [128, C.NWGU, NDC * 2 * 128], BF16) as wgu_sb,
        nc.sbuf_tensor("wd_sb", [128, 2, G * D], BF16) as wd_sb,
        nc.sbuf_tensor("gT", [128, 2, G, T], BF16) as gT,
        nc.sbuf_tensor("silu", [128, 2, 512], F32) as silu,
        nc.sbuf_tensor("lng", [128, D], F32) as lng,
        nc.sbuf_tensor("lnb", [128, D], F32) as lnb,
        nc.sbuf_tensor("bnst", [128, NT, 24], F32) as bnst,
        nc.sbuf_tensor("mv", [128, NT, 2], F32) as mv,
        nc.sbuf_tensor("rstd", [128, NT], F32) as rstd,
        nc.sbuf_tensor("nmr", [128, NT], F32) as nmr,
        nc.psum_tensor("ps", [128, 8, 512], F32) as ps,
    ):
        C.xres, C.xT, C.xbf, C.ident = xres, xT, xbf, identb
        C.wgu, C.wd, C.gT, C.silu = wgu_sb, wd_sb, gT, silu
        C.lng, C.lnb, C.bnst, C.mv, C.rstd, C.nmr = lng, lnb, bnst, mv, rstd, nmr
        C.ps = ps
        C.psT = ps[:, :, :].bitcast(BF16)
        C.tp_bank = lambda hh: 4 + hh
        S = C.S
        S.op("pool", [], extra_deps=[B.last_cc] if B.last_cc is not None else [])
        S.op("pool", lambda e: e.dma_start(out=identb[:, :], in_=ident[:, :]), writes=["ident"], dma=True)
        emit_load_x(C, x)
        emit_make_xT(C)
        emit_ffn_ln(C, wgu, wd, lg, lb, NG)
        emit_store_x(C, y)
        _finish(nc, C)
    return nc


def _finish(nc, C):
    import contextlib
    with contextlib.ExitStack() as st:
        sems = {e: st.enter_context(nc.semaphore(f"s_{e}")) for e in Sched.ENGS}
        dma_sems = {e: [st.enter_context(nc.semaphore(f"d_{e}{i}")) for i in range(NDMASEM)] for e in ("sp", "pool", "act")}
        for e in Sched.ENGS:
            dma_sems.setdefault(e, dma_sems["sp"])
        cc_sems = [st.enter_context(nc.semaphore(f"cc{i}")) for i in range(len(C.S.cc_ops))]
        block = st.enter_context(nc.Block())
        C.S.emit(nc, block, sems, dma_sems, cc_sems)


def lay_wgu(w):
    w5 = w.reshape(NDC, 128, 2, NFC, 128)
    return np.ascontiguousarray(w5.transpose(3, 1, 0, 2, 4)).reshape(NFC, 128, NDC * 2 * 128)


def lay_wd(w):
    w4 = w.reshape(NG, G, 128, D)
    return np.ascontiguousarray(w4.transpose(0, 2, 1, 3)).reshape(NG, 128, G * D)


_PROGS = {}


def wshard(a4, c):
    if USE_CC:
        return np.ascontiguousarray(a4[:, c])
    return a4.reshape(a4.shape[0], a4.shape[1] * a4.shape[2], a4.shape[3])


def wput(m, name, a4, c):
    w = wshard(a4, c)
    for j in range(w.shape[0]):
        m[f"{name}{j}"] = w[j]
    return m


def _prog(name, fn):
    if name not in _PROGS:
        _PROGS[name] = fn()
    return _PROGS[name]


def run_ffn(xs, wgu, wd, g, b):
    nc = _prog("ffn", build_ffn_prog)
    wgl, wdl = lay_wgu(wgu), lay_wd(wd)
    ident = np.eye(128, dtype=np.float32)
    g2 = np.ascontiguousarray(g.reshape(1, D))
    b2 = np.ascontiguousarray(b.reshape(1, D))
    wgs = wgl.reshape(4, NCORES, 176, NDC * 2 * 128)
    wds = wdl.reshape(1, NCORES, 176, G * D)
    in_maps = [wput(wput({"x": xs[c], "ln_g": g2, "ln_b": b2, "ident": ident}, "wgu", wgs, c), "wd", wds, c) for c in range(NCORES)]
    res = run_bass_kernel_spmd(nc, in_maps, core_ids=list(range(NCORES)))
    return [r["y"] for r in res.results]


def build_ln_prog():
    B = Builder()
    nc, C = B.nc, B.C
    x = B.dram_in("x", [T, D])
    lg = B.dram_in("ln_g", [1, D])
    lb = B.dram_in("ln_b", [1, D])
    y = B.dram_out("y", [T, D])
    with (
        nc.sbuf_tensor("xres", [128, NT, D], F32) as xres,
        nc.sbuf_tensor("lng", [128, D], F32) as lng,
        nc.sbuf_tensor("lnb", [128, D], F32) as lnb,
        nc.sbuf_tensor("bnst", [128, NT, 24], F32) as bnst,
        nc.sbuf_tensor("mv", [128, NT, 2], F32) as mv,
        nc.sbuf_tensor("rstd", [128, NT], F32) as rstd,
        nc.sbuf_tensor("nmr", [128, NT], F32) as nmr,
    ):
        C.xres = xres
        C.lng, C.lnb, C.bnst, C.mv, C.rstd, C.nmr = lng, lnb, bnst, mv, rstd, nmr
        S = C.S
        S.op("sp", lambda e: e.dma_start(out=C.lng[:, :], in_=lg.partition_broadcast(128)), writes=["lng"], dma=True)
        S.op("sp", lambda e: e.dma_start(out=C.lnb[:, :], in_=lb.partition_broadcast(128)), writes=["lnb"], dma=True)
        emit_load_x(C, x)
        emit_ln(C)
        emit_store_x(C, y)
        _finish(nc, C)
    return nc


D_IN = 4608
NCB = D_IN // 512


def build_proj_prog():
    B = Builder()
    nc, C = B.nc, B.C
    S = C.S
    x = B.dram_in("x", [T, D])
    win = B.gathered("win", 1, NCB * 128, NDC * 512)
    cos2 = B.dram_in("cos2", [T, 128])
    sinS = B.dram_in("sinS", [T, 128])
    gq = B.dram_in("gq", [1, 128])
    gk = B.dram_in("gk", [1, 128])
    ident = B.dram_in("ident", [128, 128])
    hb = B.dram_out("hb", [T, D_IN], BF16)
    with (
        nc.sbuf_tensor("xstage", [128, 2, D], F32) as xstage,
        nc.sbuf_tensor("xbf", [128, D], BF16) as xbf,
        nc.sbuf_tensor("xT", [128, NDC, T], BF16) as xT,
        nc.sbuf_tensor("identb", [128, 128], BF16) as identb,
        nc.sbuf_tensor("wsl", [128, 2, NDC * 512], BF16) as wsl,
        nc.sbuf_tensor("hbs", [128, 2, NT, 512], BF16) as hbs,
        nc.sbuf_tensor("hq", [128, NT, 1280], F32) as hq,
        nc.sbuf_tensor("hqb", [128, NT, 1280], BF16) as hqb,
        nc.sbuf_tensor("cosb", [128, NT, 128], F32) as cosb,
        nc.sbuf_tensor("sinb", [128, NT, 128], F32) as sinb,
        nc.sbuf_tensor("gqb", [128, 128], F32) as gqb,
        nc.sbuf_tensor("gkb", [128, 128], F32) as gkb,
        nc.sbuf_tensor("junk", [128, 128], F32) as junk,
        nc.sbuf_tensor("ss", [128, NT, 10], F32) as ss,
        nc.sbuf_tensor("rs", [128, NT, 10], F32) as rs,
        nc.sbuf_tensor("yy", [128, 2, 128], F32) as yy,
        nc.sbuf_tensor("t1", [128, 2, 128], F32) as t1,
        nc.sbuf_tensor("t2", [128, 2, 128], F32) as t2,
        nc.psum_tensor("ps", [128, 8, 512], F32) as ps,
    ):
        C.ps = ps
        C.psT = ps[:, :, :].bitcast(BF16)
        S.op("pool", [], extra_deps=[B.last_cc] if B.last_cc is not None else [])
        S.op("pool", lambda e: e.dma_start(out=identb[:, :], in_=ident[:, :]), writes=["ident"], dma=True)
        S.op("sp", lambda e: e.dma_start(out=cosb[:, :, :], in_=cos2.rearrange("(n p) d -> p n d", p=128)), writes=["cosb"], dma=True)
        S.op("sp", lambda e: e.dma_start(out=sinb[:, :, :], in_=sinS.rearrange("(n p) d -> p n d", p=128)), writes=["sinb"], dma=True)
        S.op("sp", lambda e: e.dma_start(out=gqb[:, :], in_=gq.partition_broadcast(128)), writes=["gqb"], dma=True)
        S.op("sp", lambda e: e.dma_start(out=gkb[:, :], in_=gk.partition_broadcast(128)), writes=["gkb"], dma=True)
        xv = x.rearrange("(n p) d -> p n d", p=128)
        for t in range(NT):
            sl = t % 2
            S.op("sp", lambda e, t=t, sl=sl: e.dma_start(out=xstage[:, sl, :], in_=xv[:, t, :]), writes=[f"xst{sl}"], dma=True)
            S.op("act", lambda e, sl=sl: e.activation(out=xbf[:, :], in_=xstage[:, sl, :], func=AF.Copy),
                 reads=[f"xst{sl}"], writes=["xbf"])
            for hh in range(2):
                bank = 6 + hh
                pst = C.psT[:, bank, :]
                fns = [(lambda e, c=hh * 8 + k, k=k, pst=pst: e.transpose(
                    out=pst[:, k * 128:(k + 1) * 128], in_=xbf[:, c * 128:(c + 1) * 128], identity=identb[:, :])) for k in range(8)]
                S.op("pe", fns, reads=["xbf", "ident"], writes=[f"ps{bank}"])
                S.op("dve", lambda e, t=t, hh=hh, pst=pst: e.tensor_copy(
                    out=xT[:, hh * 8:(hh + 1) * 8, t * 128:(t + 1) * 128], in_=pst.rearrange("p (k j) -> p k j", k=8)),
                    reads=[f"ps{bank}"], writes=[f"xT{t}"])
        hbv = hb.rearrange("(n p) d -> p n d", p=128)
        outs = []
        pctr = 0
        for n in range(NCB):
            wslot = n % 2
            S.op("pool", lambda e, n=n, wslot=wslot: e.dma_start(out=wsl[:, wslot, :], in_=win[0][n * 128:(n + 1) * 128, :]),
                 reads=["win_g0"], writes=[f"wsl{wslot}"], dma=True)
            hs = n % 2
            for t in range(NT):
                bank = pctr % 6
                pctr += 1
                fns = [(lambda e, c=c, t=t, bank=bank, wslot=wslot: e.matmul(
                    ps[:, bank, :], lhsT=xT[:, c, t * 128:(t + 1) * 128], rhs=wsl[:, wslot, c * 512:(c + 1) * 512],
                    start=(c == 0), stop=(c == NDC - 1))) for c in range(NDC)]
                S.op("pe", fns, reads=[f"wsl{wslot}", f"xT{t}"], writes=[f"ps{bank}"])
                if n < 6:
                    S.op("act", lambda e, t=t, bank=bank, hs=hs: e.activation(out=hbs[:, hs, t, :], in_=ps[:, bank, :], func=AF.Copy),
                         reads=[f"ps{bank}"], writes=[f"hbs{hs}_{t}"])
                elif n < 8:
                    S.op("act", lambda e, t=t, bank=bank, n=n: e.activation(
                        out=hq[:, t, (n - 6) * 512:(n - 5) * 512], in_=ps[:, bank, :], func=AF.Copy),
                        reads=[f"ps{bank}"], writes=[f"hq{t}_{n}"])
                else:
                    S.op("act", lambda e, t=t, bank=bank: e.activation(out=hq[:, t, 1024:1280], in_=ps[:, bank, 0:256], func=AF.Copy),
                         reads=[f"ps{bank}"], writes=[f"hq{t}_8"])
                    S.op("act", lambda e, t=t, bank=bank, hs=hs: e.activation(out=hbs[:, hs, t, 0:256], in_=ps[:, bank, 256:512], func=AF.Copy),
                         reads=[f"ps{bank}"], writes=[f"hbs{hs}_{t}"])
            if n < 6:
                outs.append(S.op("sp", lambda e, n=n, hs=hs: e.dma_start(out=hbv[:, :, n * 512:(n + 1) * 512], in_=hbs[:, hs, :, :]),
                                 reads=[f"hbs{hs}_{t}" for t in range(NT)], dma=True))
            elif n == 8:
                outs.append(S.op("sp", lambda e, hs=hs: e.dma_start(out=hbv[:, :, 4352:4608], in_=hbs[:, hs, :, 0:256]),
                                 reads=[f"hbs{hs}_{t}" for t in range(NT)], dma=True))
        S.op("dve", lambda e: e.memset(ss[:, :, :], 0.0), writes=[f"ss{t}_{hd}" for t in range(NT) for hd in range(10)])
        for t in range(NT):
            hk = [f"hq{t}_{n}" for n in (6, 7, 8)]
            for hd in range(10):
                S.op("act", lambda e, t=t, hd=hd: e.activation(out=junk[:, :], in_=hq[:, t, hd * 128:(hd + 1) * 128], func=AF.Square,
                                                                accum_out=ss[:, t, hd:hd + 1]),
                     reads=hk, writes=["junk", f"ss{t}_{hd}"])
            ssk = [f"ss{t}_{hd}" for hd in range(10)]
            S.op("act", lambda e, t=t: e.activation(out=rs[:, t, :], in_=ss[:, t, :], func=AF.Ln, bias=float(RMS_EPS), scale=1.0 / 128),
                 reads=ssk, writes=[f"rs{t}"])
            S.op("act", lambda e, t=t: e.activation(out=rs[:, t, :], in_=rs[:, t, :], func=AF.Exp, scale=-0.5),
                 reads=[f"rs{t}"], writes=[f"rs{t}"])
            for hd in range(10):
                s2 = hd % 2
                gb = gqb if hd < 8 else gkb
                xh = hq[:, t, hd * 128:(hd + 1) * 128]
                S.op("dve", lambda e, t=t, hd=hd, s2=s2, gb=gb, xh=xh: e.scalar_tensor_tensor(
                    out=yy[:, s2, :], in0=xh, scalar=rs[:, t, hd:hd + 1], in1=gb[:, :], op0=ALU.mult, op1=ALU.mult),
                    reads=hk + [f"rs{t}", "gqb", "gkb"], writes=[f"yy{s2}"])
                S.op("dve", lambda e, t=t, s2=s2: e.tensor_tensor(out=t1[:, s2, :], in0=yy[:, s2, :], in1=cosb[:, t, :], op=ALU.mult),
                     reads=[f"yy{s2}", "cosb"], writes=[f"t1{s2}"])
                yv = yy[:, s2, :].rearrange("p (a b c) -> p a b c", a=2, b=2)
                tv = t2[:, s2, :].rearrange("p (a b c) -> p a b c", a=2, b=2)
                sv = sinb[:, t, :].rearrange("p (a b c) -> p a b c", a=2, b=2)
                S.op("dve", lambda e, yv=yv, tv=tv, sv=sv: e.tensor_tensor(out=tv[:, :, 0, :], in0=yv[:, :, 1, :], in1=sv[:, :, 0, :], op=ALU.mult),
                     reads=[f"yy{s2}", "sinb"], writes=[f"t2a{s2}"])
                S.op("dve", lambda e, yv=yv, tv=tv, sv=sv: e.tensor_tensor(out=tv[:, :, 1, :], in0=yv[:, :, 0, :], in1=sv[:, :, 1, :], op=ALU.mult),
                     reads=[f"yy{s2}", "sinb"], writes=[f"t2b{s2}"])
                S.op("dve", lambda e, t=t, hd=hd, s2=s2: e.tensor_tensor(out=hqb[:, t, hd * 128:(hd + 1) * 128], in0=t1[:, s2, :], in1=t2[:, s2, :], op=ALU.add),
                     reads=[f"t1{s2}", f"t2a{s2}", f"t2b{s2}"], writes=[f"hqb{t}"])
        outs.append(S.op("sp", lambda e: e.dma_start(out=hbv[:, :, 3072:4352], in_=hqb[:, :, :]),
                         reads=[f"hqb{t}" for t in range(NT)], dma=True))
        S.op("sp", [], extra_deps=outs)
        _finish(nc, C)
    return nc


def lay_win(w):
    w4 = w.reshape(NDC, 128, NCB, 512)
    return np.ascontiguousarray(w4.transpose(2, 1, 0, 3)).reshape(NCB * 128, NDC * 512)


def rope_tables(core):
    t = np.arange(core * T, (core + 1) * T)
    row = (t // 64).astype(np.float32)
    col = (t % 64).astype(np.float32)
    nfreq = 32
    inv = (1.0 / (10000.0 ** (np.arange(nfreq, dtype=np.float32) / nfreq))).astype(np.float32)
    ar = row[:, None] * inv[None, :]
    ac = col[:, None] * inv[None, :]
    cos2 = np.concatenate([np.cos(ar), np.cos(ar), np.cos(ac), np.cos(ac)], 1).astype(np.float32)
    sinS = np.concatenate([-np.sin(ar), np.sin(ar), -np.sin(ac), np.sin(ac)], 1).astype(np.float32)
    return cos2, sinS


def run_proj(xs, w_in, gq, gk):
    nc = _prog("proj", build_proj_prog)
    wl = lay_win(w_in).reshape(1, NCORES, NCB * 128 // NCORES, NDC * 512)
    ident = np.eye(128, dtype=np.float32)
    in_maps = []
    for c in range(NCORES):
        cos2, sinS = rope_tables(c)
        in_maps.append(wput({}, "win", wl, c) | {"x": xs[c], "cos2": cos2, "sinS": sinS,
                        "gq": np.ascontiguousarray(gq.reshape(1, 128)), "gk": np.ascontiguousarray(gk.reshape(1, 128)), "ident": ident})
    res = run_bass_kernel_spmd(nc, in_maps, core_ids=list(range(NCORES)))
    return [r["hb"] for r in res.results]


NEG = -30000.0
NAW = 1536


def build_attn_prog():
    B = Builder()
    nc, C = B.nc, B.C
    S = C.S
    x = B.dram_in("x", [T, D])
    wout = B.gathered("wout", 1, 4 * 128, NDC * 512)
    qnaT = B.dram_in("qnaT", [128, 8 * T], BF16)
    knaT = B.dram_in("knaT", [128, 8 * NAW], BF16)
    vna = B.dram_in("vna", [128, 12 * 1024], BF16)
    qgT = B.dram_in("qgT", [128, 8 * T], BF16)
    kgT = B.dram_in("kgT", [128, 2 * 8192], BF16)
    vg = B.dram_in("vg", [128, 64 * 256], BF16)
    lib = B.dram_in("lib", [8, 128, 1408])
    mK = B.dram_in("mK", [2, 128], BF16)
    mQ = B.dram_in("mQ", [2, 16 * 512], BF16)
    gnT = B.dram_in("gnT", [128, 16])
    lg = B.dram_in("ln_g", [1, D])
    lb = B.dram_in("ln_b", [1, D])
    y = B.dram_out("y", [T, D])
    import contextlib
    with contextlib.ExitStack() as _st:
        kv = _st.enter_context(nc.sbuf_tensor("kv", [128, 2 * 16384], BF16))
        qT = _st.enter_context(nc.sbuf_tensor("qT", [128, 8 * T], BF16))
        oT = _st.enter_context(nc.sbuf_tensor("oT", [128, 8, T], F32))
        catT = _st.enter_context(nc.sbuf_tensor("catT", [128, NDC, T], BF16))
        libs = _st.enter_context(nc.sbuf_tensor("libs", [128, 2, 1408], F32))
        tmp = _st.enter_context(nc.sbuf_tensor("tmp", [128, 2, 512], F32))
        pT = _st.enter_context(nc.sbuf_tensor("pT", [128, 3, 512], BF16))
        sq = _st.enter_context(nc.sbuf_tensor("sq", [128, 2, 512], BF16))
        rinv = _st.enter_context(nc.sbuf_tensor("rinv", [128, 512], F32))
        rstdb = _st.enter_context(nc.sbuf_tensor("rstdb", [128, 2, 512], F32))
        ones = _st.enter_context(nc.sbuf_tensor("ones", [128, 128], BF16))
        mKs = _st.enter_context(nc.sbuf_tensor("mKs", [2, 128], BF16))
        mQs = _st.enter_context(nc.sbuf_tensor("mQs", [2, 16 * 512], BF16))
        gns = _st.enter_context(nc.sbuf_tensor("gns", [128, 16], F32))
        lng = _st.enter_context(nc.sbuf_tensor("lng", [128, D], F32))
        lnb = _st.enter_context(nc.sbuf_tensor("lnb", [128, D], F32))
        bnst = _st.enter_context(nc.sbuf_tensor("bnst", [128, NT, 24], F32))
        mv = _st.enter_context(nc.sbuf_tensor("mv", [128, NT, 2], F32))
        rstd = _st.enter_context(nc.sbuf_tensor("rstd", [128, NT], F32))
        nmr = _st.enter_context(nc.sbuf_tensor("nmr", [128, NT], F32))
        ps = _st.enter_context(nc.psum_tensor("ps", [128, 8, 512], F32))
        C.ps = ps
        wos = oT[:, :, :].rearrange("p h t -> p (h t)").bitcast(BF16).rearrange("p (s n) -> p s n", s=2)
        KT = kv[:, 0:16384]
        V = kv[:, 16384:32768]
        C.xres = kv[:, :].bitcast(F32).rearrange("p (n d) -> p n d", n=NT)
        C.lng, C.lnb, C.bnst, C.mv, C.rstd, C.nmr = lng, lnb, bnst, mv, rstd, nmr
        S.op("pool", [], extra_deps=[B.last_cc] if B.last_cc is not None else [])
        S.op("pool", lambda e: e.memset(ones[:, :], 1.0), writes=["ones"])
        S.op("sp", lambda e: e.dma_start(out=mKs[:, :], in_=mK[:, :]), writes=["mK"], dma=True)
        S.op("sp", lambda e: e.dma_start(out=mQs[:, :], in_=mQ[:, :]), writes=["mQ"], dma=True)
        S.op("sp", lambda e: e.dma_start(out=gns[:, :], in_=gnT[:, :]), writes=["gns"], dma=True)
        S.op("sp", lambda e: e.dma_start(out=lng[:, :], in_=lg.partition_broadcast(128)), writes=["lng"], dma=True)
        S.op("sp", lambda e: e.dma_start(out=lnb[:, :], in_=lb.partition_broadcast(128)), writes=["lnb"], dma=True)

        state = {"sb": 0, "pt": 0, "tm": 0, "ob": 0}
        scale = 128.0 ** -0.5

        def attn_block(q_ap, chunks, out_ap, out_key):
            ob = state["ob"] % 2
            state["ob"] += 1
            bo, br = 4 + ob * 2, 5 + ob * 2
            n = len(chunks)
            pend = []

            def issue_s(j):
                kT_ap, v_ap, bias_ap, mq_ap, rk = chunks[j]
                sb = state["sb"] % 3
                state["sb"] += 1
                fns = [lambda e, sb=sb, kT_ap=kT_ap: e.matmul(ps[:, sb, :], lhsT=kT_ap, rhs=q_ap, start=True, stop=(mq_ap is None))]
                if mq_ap is not None:
                    fns.append(lambda e, sb=sb, mq_ap=mq_ap: e.matmul(ps[:, sb, :], lhsT=mKs[:, :], rhs=mq_ap, start=False, stop=True))
                S.op("pe", fns, reads=rk + ["qT", "mK", "mQ"], writes=[f"ps{sb}"])
                pt = state["pt"] % 3
                state["pt"] += 1
                if bias_ap is not None:
                    tm = state["tm"] % 2
                    state["tm"] += 1
                    S.op("dve", lambda e, sb=sb, tm=tm, bias_ap=bias_ap: e.scalar_tensor_tensor(
                        out=tmp[:, tm, :], in0=ps[:, sb, :], scalar=float(scale), in1=bias_ap, op0=ALU.mult, op1=ALU.add),
                        reads=[f"ps{sb}", "lib"], writes=[f"tmp{tm}"])
                    S.op("act", lambda e, tm=tm, pt=pt: e.activation(out=pT[:, pt, :], in_=tmp[:, tm, :], func=AF.Exp),
                         reads=[f"tmp{tm}"], writes=[f"pT{pt}"])
                else:
                    S.op("act", lambda e, sb=sb, pt=pt: e.activation(out=pT[:, pt, :], in_=ps[:, sb, :], func=AF.Exp, scale=float(scale)),
                         reads=[f"ps{sb}"], writes=[f"pT{pt}"])
                return pt

            def issue_pv(j, pt):
                kT_ap, v_ap, bias_ap, mq_ap, rk = chunks[j]
                S.op("pe", [lambda e, pt=pt, v_ap=v_ap: e.matmul(ps[:, bo, :], lhsT=v_ap, rhs=pT[:, pt, :], start=(j == 0), stop=(j == n - 1)),
                            lambda e, pt=pt: e.matmul(ps[:, br, :], lhsT=ones[:, :], rhs=pT[:, pt, :], start=(j == 0), stop=(j == n - 1))],
                     reads=rk + [f"pT{pt}", "ones"], writes=[f"ps{bo}", f"ps{br}"])

            pts = {}
            pts[0] = issue_s(0)
            for j in range(n):
                if j + 1 < n:
                    pts[j + 1] = issue_s(j + 1)
                issue_pv(j, pts[j])
            S.op("dve", lambda e: e.reciprocal(out=rinv[:, :], in_=ps[:, br, :]), reads=[f"ps{br}"], writes=["rinv"])
            S.op("dve", lambda e: e.tensor_tensor(out=out_ap, in0=ps[:, bo, :], in1=rinv[:, :], op=ALU.mult),
                 reads=[f"ps{bo}", "rinv"], writes=[out_key])

        def group_norm(grp):
            for qb in range(2):
                for h in range(8):
                    s2 = h % 2
                    S.op("act", lambda e, h=h, qb=qb, s2=s2: e.activation(out=sq[:, s2, :], in_=oT[:, h, qb * 512:(qb + 1) * 512], func=AF.Square),
                         reads=[f"oT{h}_{qb}"], writes=[f"sq{s2}"])
                    S.op("pe", lambda e, h=h, s2=s2: e.matmul(ps[:, 3, :], lhsT=ones[:, :], rhs=sq[:, s2, :], start=(h == 0), stop=(h == 7)),
                         reads=[f"sq{s2}", "ones"], writes=["ps3"])
                S.op("act", lambda e, qb=qb: e.activation(out=rstdb[:, qb, :], in_=ps[:, 3, :], func=AF.Ln, bias=float(RMS_EPS), scale=1.0 / 1024),
                     reads=["ps3"], writes=[f"rstdb{qb}"])
                S.op("act", lambda e, qb=qb: e.activation(out=rstdb[:, qb, :], in_=rstdb[:, qb, :], func=AF.Exp, scale=-0.5),
                     reads=[f"rstdb{qb}"], writes=[f"rstdb{qb}"])
                for h in range(8):
                    c = grp * 8 + h
                    S.op("dve", lambda e, h=h, qb=qb, c=c: e.scalar_tensor_tensor(
                        out=catT[:, c, qb * 512:(qb + 1) * 512], in0=oT[:, h, qb * 512:(qb + 1) * 512], scalar=gns[:, c:c + 1],
                        in1=rstdb[:, qb, :], op0=ALU.mult, op1=ALU.mult),
                        reads=[f"oT{h}_{qb}", f"rstdb{qb}", "gns"], writes=[f"catT{c}_{qb}"])

        S.op("sp", lambda e: e.dma_start(out=qT[:, :], in_=qnaT[:, :]), writes=["qT"], dma=True)
        for h in range(8):
            S.op("sp", lambda e, h=h: e.dma_start(out=KT[:, h * NAW:(h + 1) * NAW], in_=knaT[:, h * NAW:(h + 1) * NAW]),
                 writes=[f"KT{h}"], dma=True)
        for tt in range(12):
            S.op("sp", lambda e, tt=tt: e.dma_start(out=V[:, tt * 1024:(tt + 1) * 1024], in_=vna[:, tt * 1024:(tt + 1) * 1024]),
                 writes=[f"V{tt}"], dma=True)
        for h in range(8):
            ls = h % 2
            S.op("sp", lambda e, h=h, ls=ls: e.dma_start(out=libs[:, ls, :], in_=lib[h, :, :]), writes=["lib"], dma=True)
            for qb in range(2):
                chunks = []
                for j in range(8):
                    tok0 = (qb * 8 + 2 * j) * 64
                    tt = tok0 // 128
                    chunks.append((KT[:, h * NAW + tok0:h * NAW + tok0 + 128],
                                   V[:, tt * 1024 + h * 128:tt * 1024 + (h + 1) * 128],
                                   libs[:, ls, (14 - 2 * j) * 64:(22 - 2 * j) * 64],
                                   mQs[:, (qb * 8 + j) * 512:(qb * 8 + j + 1) * 512],
                                   [f"KT{h}", f"V{tt}"]))
                attn_block(qT[:, h * T + qb * 512:h * T + (qb + 1) * 512], chunks, oT[:, h, qb * 512:(qb + 1) * 512], f"oT{h}_{qb}")
        group_norm(0)

        S.op("sp", lambda e: e.dma_start(out=qT[:, :], in_=qgT[:, :]), writes=["qT"], dma=True)
        ktk = [f"KT{h}" for h in range(8)]
        vtk = [f"V{tt}" for tt in range(12)]
        kgk = []
        for kh in range(2):
            for part in range(4):
                kgk.append(f"KG{kh}_{part}")
                S.op("sp", lambda e, kh=kh, part=part: e.dma_start(
                    out=KT[:, kh * 8192 + part * 2048:kh * 8192 + (part + 1) * 2048],
                    in_=kgT[:, kh * 8192 + part * 2048:kh * 8192 + (part + 1) * 2048]),
                    writes=[kgk[-1]] + ktk, dma=True)
        S.op("sp", lambda e: e.dma_start(out=V[:, 0:8192], in_=vg[:, 0:8192]), writes=["VG0"] + vtk, dma=True)
        S.op("sp", lambda e: e.dma_start(out=V[:, 8192:16384], in_=vg[:, 8192:16384]), writes=["VG1"] + vtk, dma=True)
        gkeys = kgk + ["VG0", "VG1"]
        for h in range(8):
            kh = h // 4
            for qb in range(2):
                chunks = []
                for j in range(64):
                    chunks.append((KT[:, kh * 8192 + j * 128:kh * 8192 + (j + 1) * 128],
                                   V[:, j * 256 + kh * 128:j * 256 + (kh + 1) * 128], None, None, gkeys))
                attn_block(qT[:, h * T + qb * 512:h * T + (qb + 1) * 512], chunks, oT[:, h, qb * 512:(qb + 1) * 512], f"oT{h}_{qb}")
        group_norm(1)

        allk = ktk + vtk + gkeys
        xv = x.rearrange("(n p) d -> p n d", p=128)
        for t in range(NT):
            S.op("sp", lambda e, t=t: e.dma_start(out=C.xres[:, t, :], in_=xv[:, t, :]),
                 writes=[f"xres{t}_{b}" for b in range(4)] + (allk if t == 0 else []), dma=True,
                 extra_deps=[S.ops["pe"][-1]])
        for t in range(NT):
            S.op("pool", lambda e, t=t: e.tensor_scalar(out=C.xres[:, t, :], in0=C.xres[:, t, :], scalar1=float(ALPHA),
                                                         scalar2=None, op0=ALU.mult),
                 reads=[f"xres{t}_{b}" for b in range(4)], writes=[f"xres{t}_{b}" for b in range(4)])
        pc = 0
        for b in range(4):
            ws = b % 2
            S.op("pool", lambda e, b=b, ws=ws: e.dma_start(out=wos[:, ws, :], in_=wout[0][b * 128:(b + 1) * 128, :]),
                 reads=["wout_g0"], writes=[f"wos{ws}"] + [f"oT{h}_{qb}" for h in range(8) for qb in range(2)], dma=True)
            for t in range(NT):
                bank = pc % 3
                pc += 1
                fns = [(lambda e, c=c, t=t, bank=bank, ws=ws: e.matmul(
                    ps[:, bank, :], lhsT=catT[:, c, t * 128:(t + 1) * 128], rhs=wos[:, ws, c * 512:(c + 1) * 512],
                    start=(c == 0), stop=(c == NDC - 1))) for c in range(NDC)]
                S.op("pe", fns, reads=[f"wos{ws}"] + [f"catT{c}_{t // 4}" for c in range(NDC)], writes=[f"ps{bank}"])
                S.op("dve", lambda e, t=t, b=b, bank=bank: e.tensor_tensor(
                    out=C.xres[:, t, b * 512:(b + 1) * 512], in0=ps[:, bank, :], in1=C.xres[:, t, b * 512:(b + 1) * 512], op=ALU.add),
                    reads=[f"ps{bank}", f"xres{t}_{b}"], writes=[f"xres{t}_{b}"])
        emit_ln(C)
        emit_store_x(C, y)
        _finish(nc, C)
    return nc


def na_lib(rel_bias):
    c = np.arange(64)
    kc = np.arange(64)
    c0 = np.clip(c - 8, 0, 48)
    col_in = (kc[:, None] >= c0[None, :]) & (kc[:, None] < c0[None, :] + 16)
    dcol = np.clip(kc[:, None] - c[None, :], -15, 15) + 15
    lib = np.zeros((8, 2, 64, 22, 64), np.float32)
    for s in range(22):
        for u in range(2):
            dr = 10 - s + u
            if -7 <= dr <= 7:
                vals = rel_bias[:, dr + 7][:, dcol]
                lib[:, u, :, s, :] = np.where(col_in[None], vals, np.float32(NEG))
    return np.ascontiguousarray(lib.reshape(8, 128, 22 * 64))


def na_rowmask(core):
    rows = 128
    mq = np.zeros((2, 2, 8, 8, 64), np.float32)
    for qb in range(2):
        for j in range(8):
            for u in range(2):
                kr = core * 16 + qb * 8 - 4 + 2 * j + u
                for rq in range(8):
                    qr = core * 16 + qb * 8 + rq
                    r0 = min(max(qr - 4, 0), rows - 8)
                    ok = (0 <= kr < rows) and (r0 <= kr < r0 + 8)
                    mq[u, qb, j, rq, :] = 0.0 if ok else NEG
    mK = np.zeros((2, 2, 64), np.float32)
    mK[0, 0, :] = 1.0
    mK[1, 1, :] = 1.0
    return mK.reshape(2, 128).astype(ml_dtypes.bfloat16), mq.reshape(2, 16 * 512).astype(ml_dtypes.bfloat16)


def lay_wout(w):
    w4 = w.reshape(NDC, 128, 4, 512)
    return np.ascontiguousarray(w4.transpose(2, 1, 0, 3)).reshape(4 * 128, NDC * 512)


def run_attn(xs, hb, w_out, rel_bias, gn_na, gn_gqa, g, b):
    nc = _prog("attn", build_attn_prog)
    wl = lay_wout(w_out).reshape(1, NCORES, 64, NDC * 512)
    lib = na_lib(rel_bias)
    gnT = np.ascontiguousarray(np.concatenate([gn_na, gn_gqa]).reshape(16, 128).T)
    zero_pad = np.zeros((256, 1024), hb.dtype)
    kna_p = np.concatenate([zero_pad, hb[:, 1024:2048], zero_pad], 0)
    vna_p = np.concatenate([zero_pad, hb[:, 2048:3072], zero_pad], 0)
    kgT = np.ascontiguousarray(hb[:, 4096:4352].reshape(8192, 2, 128).transpose(2, 1, 0)).reshape(128, 2 * 8192)
    vgl = np.ascontiguousarray(hb[:, 4352:4608].reshape(64, 128, 256).transpose(1, 0, 2)).reshape(128, 64 * 256)
    in_maps = []
    for c in range(NCORES):
        t0 = c * T
        qna = hb[t0:t0 + T, 0:1024]
        qg = hb[t0:t0 + T, 3072:4096]
        kwin = kna_p[t0:t0 + NAW]
        vwin = vna_p[t0:t0 + NAW]
        mK, mQ = na_rowmask(c)
        in_maps.append(wput({}, "wout", wl, c) | {
            "x": xs[c],
            "qnaT": np.ascontiguousarray(qna.reshape(T, 8, 128).transpose(2, 1, 0)).reshape(128, 8 * T),
            "knaT": np.ascontiguousarray(kwin.reshape(NAW, 8, 128).transpose(2, 1, 0)).reshape(128, 8 * NAW),
            "vna": np.ascontiguousarray(vwin.reshape(12, 128, 1024).transpose(1, 0, 2)).reshape(128, 12 * 1024),
            "qgT": np.ascontiguousarray(qg.reshape(T, 8, 128).transpose(2, 1, 0)).reshape(128, 8 * T),
            "kgT": kgT, "vg": vgl, "lib": lib, "mK": mK, "mQ": mQ, "gnT": gnT,
            "ln_g": np.ascontiguousarray(g.reshape(1, D)), "ln_b": np.ascontiguousarray(b.reshape(1, D)),
        })
    res = run_bass_kernel_spmd(nc, in_maps, core_ids=list(range(NCORES)))
    return [r["y"] for r in res.results]


def kernel(x, ffn1_w_gate_up, ffn1_w_down, ln1_g, ln1_b, w_in, na_rel_bias, q_norm_g, k_norm_g, gn_na_g, gn_gqa_g,
           w_out, ln2_g, ln2_b, ffn2_w_gate_up, ffn2_w_down, ln3_g, ln3_b):
    f = lambda a: np.asarray(a, dtype=np.float32)
    xs = [np.ascontiguousarray(f(x)[0, c * T:(c + 1) * T]) for c in range(NCORES)]
    for l in range(DEPTH):
        xs = run_ffn(xs, f(ffn1_w_gate_up)[l], f(ffn1_w_down)[l], f(ln1_g)[l], f(ln1_b)[l])
        hbs = run_proj(xs, f(w_in)[l], f(q_norm_g)[l], f(k_norm_g)[l])
        hb = np.concatenate(hbs, 0)
        xs = run_attn(xs, hb, f(w_out)[l], f(na_rel_bias)[l], f(gn_na_g)[l], f(gn_gqa_g)[l], f(ln2_g)[l], f(ln2_b)[l])
        xs = run_ffn(xs, f(ffn2_w_gate_up)[l], f(ffn2_w_down)[l], f(ln3_g)[l], f(ln3_b)[l])
    return np.concatenate(xs, 0)[None].astype(np.float32)
```

```python
import numpy as np
import ml_dtypes
import concourse.bass as bass
import concourse.mybir as mybir
from concourse.bass_utils import run_bass_kernel_spmd

F32 = mybir.dt.float32
BF16 = mybir.dt.bfloat16
AF = mybir.ActivationFunctionType
ALU = mybir.AluOpType
AX = mybir.AxisListType

NCORES = 8
D = 2048
DFF = 5632
T = 1024
NT = T // 128
NDC = D // 128
NFC = DFF // 128
G = 4
NG = NFC // G
DEPTH = 2
ALPHA = (2.0 * DEPTH) ** 0.25
LN_EPS = 1.0001e-5
RMS_EPS = 1e-6
NDMASEM = 12
USE_CC = False


class Op:
    __slots__ = ("eng", "fns", "deps", "sig", "dma", "dma_idx", "val", "sem", "cc")

    def __init__(self, eng, fns, dma):
        self.eng = eng
        self.fns = fns
        self.dma = dma
        self.deps = ()
        self.sig = False
        self.dma_idx = -1
        self.val = 0
        self.sem = None
        self.cc = False


class Sched:
    ENGS = ("pe", "act", "dve", "pool", "sp")

    def __init__(self):
        self.ops = {e: [] for e in self.ENGS}
        self.last_w = {}
        self.readers = {}
        self.dma_hist = {e: [] for e in self.ENGS}
        self.cc_ops = []

    def op(self, eng, fns, reads=(), writes=(), dma=False, extra_deps=(), cc=False):
        if not isinstance(fns, (list, tuple)):
            fns = [fns]
        o = Op(eng, list(fns), dma)
        deps = set(extra_deps)
        for k in reads:
            w = self.last_w.get(k)
            if w is not None:
                deps.add(w)
        for k in writes:
            w = self.last_w.get(k)
            if w is not None:
                deps.add(w)
            for r in self.readers.get(k, ()):
                deps.add(r)
        for k in reads:
            self.readers.setdefault(k, []).append(o)
        for k in writes:
            self.last_w[k] = o
            self.readers[k] = []
        if cc:
            o.cc = True
            self.cc_ops.append(o)
        if dma:
            hist = self.dma_hist[eng]
            if len(hist) >= NDMASEM:
                deps.add(hist[-NDMASEM])
            o.dma_idx = len(hist)
            hist.append(o)
        deps.discard(o)
        o.deps = [d for d in deps if not (d.eng == "pe" and eng == "pe" and not d.dma)]
        for d in o.deps:
            d.sig = True
        self.ops[eng].append(o)
        return o

    def emit(self, nc, block, sems, dma_sems, cc_sems=()):
        for i, o in enumerate(self.cc_ops):
            o.sem = cc_sems[i]
            o.val = 1
        for e in self.ENGS:
            cnt = 0
            for o in self.ops[e]:
                if o.cc:
                    continue
                if o.dma:
                    o.sem = dma_sems[e][o.dma_idx % NDMASEM]
                    o.val = 16 * (o.dma_idx // NDMASEM + 1)
                elif o.sig:
                    cnt += 1
                    o.sem = sems[e]
                    o.val = cnt

        def run(e, engine):
            seen = {}
            for o in self.ops[e]:
                for d in o.deps:
                    key = id(d.sem)
                    if seen.get(key, 0) < d.val:
                        engine.wait_ge(d.sem, d.val)
                        seen[key] = d.val
                ins = None
                for fn in o.fns:
                    ins = fn(engine)
                if o.cc:
                    ins.then_inc(o.sem)
                elif o.dma:
                    ins.then_inc(o.sem, 16)
                elif o.sig:
                    assert ins is not None
                    ins.then_inc(o.sem, 1)

        @block.tensor
        def _(eng):
            run("pe", eng)

        @block.scalar
        def _(eng):
            run("act", eng)

        @block.vector
        def _(eng):
            run("dve", eng)

        @block.gpsimd
        def _(eng):
            run("pool", eng)

        @block.sync
        def _(eng):
            run("sp", eng)


class Ctx:
    pass


def emit_load_x(C, x_dram):
    S = C.S
    xv = x_dram.rearrange("(n p) d -> p n d", p=128)
    for t in range(NT):
        S.op("sp", lambda e, t=t: e.dma_start(out=C.xres[:, t, :], in_=xv[:, t, :]),
             writes=[f"xres{t}_{b}" for b in range(4)], dma=True)


def emit_make_xT(C):
    S = C.S
    for t in range(NT):
        S.op("act", lambda e, t=t: e.activation(out=C.xbf[:, :], in_=C.xres[:, t, :], func=AF.Copy),
             reads=[f"xres{t}_{b}" for b in range(4)], writes=["xbf"])
        for hh in range(2):
            bank = C.tp_bank(hh)
            pst = C.psT[:, bank, :]
            fns = []
            for k in range(8):
                c = hh * 8 + k
                fns.append(lambda e, c=c, k=k, pst=pst: e.transpose(
                    out=pst[:, k * 128:(k + 1) * 128], in_=C.xbf[:, c * 128:(c + 1) * 128], identity=C.ident[:, :]))
            S.op("pe", fns, reads=["xbf", "ident"], writes=[f"ps{bank}"])
            S.op("dve", lambda e, t=t, hh=hh, pst=pst: e.tensor_copy(
                out=C.xT[:, hh * 8:(hh + 1) * 8, t * 128:(t + 1) * 128],
                in_=pst.rearrange("p (k j) -> p k j", k=8)),
                reads=[f"ps{bank}"], writes=[f"xT{t}"])


def emit_ffn_ln(C, wgu_dram, wd_dram, g_dram, b_dram, NG=NG, wname="w"):
    S = C.S
    S.op("sp", lambda e: e.dma_start(out=C.lng[:, :], in_=g_dram.partition_broadcast(128)), writes=["lng"], dma=True)
    S.op("sp", lambda e: e.dma_start(out=C.lnb[:, :], in_=b_dram.partition_broadcast(128)), writes=["lnb"], dma=True)

    for t in range(NT):
        S.op("dve", lambda e, t=t: e.tensor_scalar(out=C.xres[:, t, :], in0=C.xres[:, t, :], scalar1=float(ALPHA),
                                                    scalar2=None, op0=ALU.mult),
             reads=[f"xres{t}_{b}" for b in range(4)], writes=[f"xres{t}_{b}" for b in range(4)])

    def load_wgu(i):
        slot = C.wgu_ctr % C.NWGU
        C.wgu_ctr += 1
        j, r0 = i // 11, (i % 11) * 128
        S.op("pool", lambda e: e.dma_start(out=C.wgu[:, slot, :], in_=wgu_dram[j][r0:r0 + 128, :]),
             reads=[f"{wname}gu_g{j}"], writes=[f"wgu{slot}"], dma=True)
        return slot

    def load_wd(g):
        slot = C.wd_ctr % 2
        C.wd_ctr += 1
        S.op("pool", lambda e: e.dma_start(out=C.wd[:, slot, :], in_=wd_dram[0][g * 128:(g + 1) * 128, :]),
             reads=[f"{wname}d_g0"], writes=[f"wd{slot}"], dma=True)
        return slot

    unit_ctr = [0]

    def up_group(g, wslots):
        gs = g % 2
        for q in range(G):
            ws = wslots[q]
            for h in range(2):
                u = unit_ctr[0]
                unit_ctr[0] += 1
                bg, bu = (0, 1) if u % 2 == 0 else (2, 3)
                xkeys = [f"xT{t}" for t in range(h * 4, h * 4 + 4)]
                for (bank, s) in ((bg, 0), (bu, 1)):
                    fns = []
                    for c in range(NDC):
                        fns.append(lambda e, c=c, bank=bank, s=s, ws=ws, h=h: e.matmul(
                            C.ps[:, bank, :], lhsT=C.wgu[:, ws, (c * 2 + s) * 128:(c * 2 + s + 1) * 128],
                            rhs=C.xT[:, c, h * 512:(h + 1) * 512], start=(c == 0), stop=(c == NDC - 1)))
                    S.op("pe", fns, reads=[f"wgu{ws}"] + xkeys, writes=[f"ps{bank}"])
                sl = u % 2
                S.op("act", lambda e, bg=bg, sl=sl: e.activation(out=C.silu[:, sl, :], in_=C.ps[:, bg, :], func=AF.Silu),
                     reads=[f"ps{bg}"], writes=[f"silu{sl}"])
                S.op("dve", lambda e, bu=bu, sl=sl, gs=gs, q=q, h=h: e.tensor_tensor(
                    out=C.gT[:, gs, q, h * 512:(h + 1) * 512], in0=C.ps[:, bu, :], in1=C.silu[:, sl, :], op=ALU.mult),
                    reads=[f"ps{bu}", f"silu{sl}"], writes=[f"gT{gs}_{q}_{h}"])

    dctr = [0]

    def down_group(g, wdslot):
        gs = g % 2
        for t in range(NT):
            h = t // 4
            for b in range(4):
                bank = 4 + dctr[0] % 4
                dctr[0] += 1
                fns = []
                for q in range(G):
                    fns.append(lambda e, q=q, bank=bank, t=t, b=b: e.matmul(
                        C.ps[:, bank, :], lhsT=C.gT[:, gs, q, t * 128:(t + 1) * 128],
                        rhs=C.wd[:, wdslot, q * D + b * 512:q * D + (b + 1) * 512], start=(q == 0), stop=(q == G - 1)))
                S.op("pe", fns, reads=[f"wd{wdslot}"] + [f"gT{gs}_{q}_{h}" for q in range(G)], writes=[f"ps{bank}"])
                S.op("dve", lambda e, bank=bank, t=t, b=b: e.scalar_tensor_tensor(
                    out=C.xres[:, t, b * 512:(b + 1) * 512], in0=C.ps[:, bank, :], scalar=0.5,
                    in1=C.xres[:, t, b * 512:(b + 1) * 512], op0=ALU.mult, op1=ALU.add),
                    reads=[f"ps{bank}", f"xres{t}_{b}"], writes=[f"xres{t}_{b}"])

    wsl = {}
    wdl = {}
    wsl[0] = [load_wgu(q) for q in range(G)]
    wdl[0] = load_wd(0)
    up_group(0, wsl[0])
    for g in range(NG):
        if g + 1 < NG:
            wsl[g + 1] = [load_wgu((g + 1) * G + q) for q in range(G)]
            wdl[g + 1] = load_wd(g + 1)
            up_group(g + 1, wsl[g + 1])
        down_group(g, wdl[g])

    emit_ln(C)


def emit_ln(C):
    S = C.S
    for t in range(NT):
        for b in range(4):
            S.op("dve", lambda e, t=t, b=b: e.bn_stats(out=C.bnst[:, t, b * 6:(b + 1) * 6], in_=C.xres[:, t, b * 512:(b + 1) * 512]),
                 reads=[f"xres{t}_{b}"], writes=[f"bnst{t}_{b}"])
    for t in range(NT):
        S.op("dve", lambda e, t=t: e.bn_aggr(out=C.mv[:, t, :], in_=C.bnst[:, t, :]),
             reads=[f"bnst{t}_{b}" for b in range(4)], writes=[f"mv{t}"])
    mvk = [f"mv{t}" for t in range(NT)]
    S.op("act", lambda e: e.activation(out=C.rstd[:, :], in_=C.mv[:, :, 1], func=AF.Ln, bias=float(LN_EPS), scale=1.0),
         reads=mvk, writes=["rstd"])
    S.op("act", lambda e: e.activation(out=C.rstd[:, :], in_=C.rstd[:, :], func=AF.Exp, scale=-0.5),
         reads=["rstd"], writes=["rstd"])
    S.op("dve", lambda e: e.scalar_tensor_tensor(out=C.nmr[:, :], in0=C.mv[:, :, 0], scalar=-1.0, in1=C.rstd[:, :],
                                                 op0=ALU.mult, op1=ALU.mult), reads=mvk + ["rstd"], writes=["nmr"])
    for t in range(NT):
        xk = [f"xres{t}_{b}" for b in range(4)]
        S.op("dve", lambda e, t=t: e.tensor_scalar(out=C.xres[:, t, :], in0=C.xres[:, t, :], scalar1=C.mv[:, t, 0:1],
                                                   scalar2=C.rstd[:, t:t + 1], op0=ALU.subtract, op1=ALU.mult),
             reads=xk + mvk + ["rstd"], writes=xk)
        S.op("dve", lambda e, t=t: e.tensor_tensor(out=C.xres[:, t, :], in0=C.xres[:, t, :], in1=C.lng[:, :], op=ALU.mult),
             reads=xk + ["lng"], writes=xk)
        S.op("dve", lambda e, t=t: e.tensor_tensor(out=C.xres[:, t, :], in0=C.xres[:, t, :], in1=C.lnb[:, :], op=ALU.add),
             reads=xk + ["lnb"], writes=xk)


def emit_store_x(C, y_dram):
    S = C.S
    yv = y_dram.rearrange("(n p) d -> p n d", p=128)
    outs = []
    for t in range(NT):
        outs.append(S.op("sp", lambda e, t=t: e.dma_start(out=yv[:, t, :], in_=C.xres[:, t, :]),
                         reads=[f"xres{t}_{b}" for b in range(4)], dma=True))
    S.op("sp", [], extra_deps=outs)


class Builder:
    def __init__(self):
        self.nc = bass.Bass("TRN2", target_bir_lowering=False)
        self.C = Ctx()
        self.C.S = Sched()
        self.C.nc = self.nc

    def dram_in(self, name, shape, dt=F32):
        return self.nc.dram_tensor(name, list(shape), dt, kind="ExternalInput").ap()

    def gathered(self, name, nchunks, rows_per_chunk, cols, dt=F32):
        nc, S = self.nc, self.C.S
        if not USE_CC:
            self.last_cc = None
            return [nc.dram_tensor(f"{name}{j}", [rows_per_chunk, cols], dt, kind="ExternalInput").ap() for j in range(nchunks)]
        rpc = rows_per_chunk // NCORES
        assert rpc * NCORES == rows_per_chunk
        fulls = []
        for j in range(nchunks):
            ext = nc.dram_tensor(f"{name}{j}", [rpc, cols], dt, kind="ExternalInput")
            bnc = nc.dram_tensor(f"{name}{j}_bnc", [rpc, cols], dt)
            full = nc.dram_tensor(f"{name}{j}_full", [rows_per_chunk, cols], dt)
            prev = [self.last_cc] if getattr(self, "last_cc", None) is not None else []
            S.op("pool", lambda e, bnc=bnc, ext=ext: e.dma_start(out=bnc[:, :], in_=ext[:, :]), writes=[f"{name}_b{j}"],
                 dma=True, extra_deps=prev)
            self.last_cc = S.op("pool", lambda e, bnc=bnc, full=full: e.collective_compute(
                "AllGather", ALU.bypass, replica_groups=[list(range(NCORES))],
                ins=[bnc.ap().opt()], outs=[full.ap().opt()]),
                reads=[f"{name}_b{j}"], writes=[f"{name}_g{j}"], cc=True)
            fulls.append(full.ap())
        return fulls

    def dram_out(self, name, shape, dt=F32):
        return self.nc.dram_tensor(name, list(shape), dt, kind="ExternalOutput").ap()


def build_ffn_prog(NG=NG):
    B = Builder()
    nc, C = B.nc, B.C
    x = B.dram_in("x", [T, D])
    wgu = B.gathered("wgu", 4, 11 * 128, NDC * 2 * 128)
    wd = B.gathered("wd", 1, 11 * 128, G * D)
    lg = B.dram_in("ln_g", [1, D])
    lb = B.dram_in("ln_b", [1, D])
    ident = B.dram_in("ident", [128, 128])
    y = B.dram_out("y", [T, D])
    C.NWGU = 4
    C.wgu_ctr = 0
    C.wd_ctr = 0
    with (
        nc.sbuf_tensor("xres", [128, NT, D], F32) as xres,
        nc.sbuf_tensor("xT", [128, NDC, T], BF16) as xT,
        nc.sbuf_tensor("xbf", [128, D], BF16) as xbf,
        nc.sbuf_tensor("identb", [128, 128], BF16) as identb,
        nc.sbuf_tensor("wgu_sb", [128, C.NWGU, NDC * 2 * 128], BF16) as wgu_sb,
        nc.sbuf_tensor("wd_sb", [128, 2, G * D], BF16) as wd_sb,
        nc.sbuf_tensor("gT", [128, 2, G, T], BF16) as gT,
        nc.sbuf_tensor("silu", [128, 2, 512], F32) as silu,
        nc.sbuf_tensor("lng", [128, D], F32) as lng,
        nc.sbuf_tensor("lnb", [128, D], F32) as lnb,
        nc.sbuf_tensor("bnst", [128, NT, 24], F32) as bnst,
        nc.sbuf_tensor("mv", [128, NT, 2], F32) as mv,
        nc.sbuf_tensor("rstd", [128, NT], F32) as rstd,
        nc.sbuf_tensor("nmr", [128, NT], F32) as nmr,
        nc.psum_tensor("ps", [128, 8, 512], F32) as ps,
    ):
        C.xres, C.xT, C.xbf, C.ident = xres, xT, xbf, identb
        C.wgu, C.wd, C.gT, C.silu = wgu_sb, wd_sb, gT, silu
        C.lng, C.lnb, C.bnst, C.mv, C.rstd, C.nmr = lng, lnb, bnst, mv, rstd, nmr
        C.ps = ps
        C.psT = ps[:, :, :].bitcast(BF16)
        C.tp_bank = lambda hh: 4 + hh
        S = C.S
        S.op("pool", [], extra_deps=[B.last_cc] if B.last_cc is not None else [])
        S.op("pool", lambda e: e.dma_start(out=identb[:, :], in_=ident[:, :]), writes=["ident"], dma=True)
        emit_load_x(C, x)
        emit_make_xT(C)
        emit_ffn_ln(C, wgu, wd, lg, lb, NG)
        emit_store_x(C, y)
        _finish(nc, C)
    return nc


def _finish(nc, C):
    import contextlib
    with contextlib.ExitStack() as st:
        sems = {e: st.enter_context(nc.semaphore(f"s_{e}")) for e in Sched.ENGS}
        dma_sems = {e: [st.enter_context(nc.semaphore(f"d_{e}{i}")) for i in range(NDMASEM)] for e in ("sp", "pool", "act")}
        for e in Sched.ENGS:
            dma_sems.setdefault(e, dma_sems["sp"])
        cc_sems = [st.enter_context(nc.semaphore(f"cc{i}")) for i in range(len(C.S.cc_ops))]
        block = st.enter_context(nc.Block())
        C.S.emit(nc, block, sems, dma_sems, cc_sems)


def lay_wgu(w):
    w5 = w.reshape(NDC, 128, 2, NFC, 128)
    return np.ascontiguousarray(w5.transpose(3, 1, 0, 2, 4)).reshape(NFC, 128, NDC * 2 * 128)


def lay_wd(w):
    w4 = w.reshape(NG, G, 128, D)
    return np.ascontiguousarray(w4.transpose(0, 2, 1, 3)).reshape(NG, 128, G * D)


_PROGS = {}


def wshard(a4, c):
    if USE_CC:
        return np.ascontiguousarray(a4[:, c])
    return a4.reshape(a4.shape[0], a4.shape[1] * a4.shape[2], a4.shape[3])


def wput(m, name, a4, c):
    w = wshard(a4, c)
    for j in range(w.shape[0]):
        m[f"{name}{j}"] = w[j]
    return m


def _prog(name, fn):
    if name not in _PROGS:
        _PROGS[name] = fn()
    return _PROGS[name]


def run_ffn(xs, wgu, wd, g, b):
    nc = _prog("ffn", build_ffn_prog)
    wgl, wdl = lay_wgu(wgu), lay_wd(wd)
    ident = np.eye(128, dtype=np.float32)
    g2 = np.ascontiguousarray(g.reshape(1, D))
    b2 = np.ascontiguousarray(b.reshape(1, D))
    wgs = wgl.reshape(4, NCORES, 176, NDC * 2 * 128)
    wds = wdl.reshape(1, NCORES, 176, G * D)
    in_maps = [wput(wput({"x": xs[c], "ln_g": g2, "ln_b": b2, "ident": ident}, "wgu", wgs, c), "wd", wds, c) for c in range(NCORES)]
    res = run_bass_kernel_spmd(nc, in_maps, core_ids=list(range(NCORES)))
    return [r["y"] for r in res.results]


def build_ln_prog():
    B = Builder()
    nc, C = B.nc, B.C
    x = B.dram_in("x", [T, D])
    lg = B.dram_in("ln_g", [1, D])
    lb = B.dram_in("ln_b", [1, D])
    y = B.dram_out("y", [T, D])
    with (
        nc.sbuf_tensor("xres", [128, NT, D], F32) as xres,
        nc.sbuf_tensor("lng", [128, D], F32) as lng,
        nc.sbuf_tensor("lnb", [128, D], F32) as lnb,
        nc.sbuf_tensor("bnst", [128, NT, 24], F32) as bnst,
        nc.sbuf_tensor("mv", [128, NT, 2], F32) as mv,
        nc.sbuf_tensor("rstd", [128, NT], F32) as rstd,
        nc.sbuf_tensor("nmr", [128, NT], F32) as nmr,
    ):
        C.xres = xres
        C.lng, C.lnb, C.bnst, C.mv, C.rstd, C.nmr = lng, lnb, bnst, mv, rstd, nmr
        S = C.S
        S.op("sp", lambda e: e.dma_start(out=C.lng[:, :], in_=lg.partition_broadcast(128)), writes=["lng"], dma=True)
        S.op("sp", lambda e: e.dma_start(out=C.lnb[:, :], in_=lb.partition_broadcast(128)), writes=["lnb"], dma=True)
        emit_load_x(C, x)
        emit_ln(C)
        emit_store_x(C, y)
        _finish(nc, C)
    return nc


D_IN = 4608
NCB = D_IN // 512


def build_proj_prog():
    B = Builder()
    nc, C = B.nc, B.C
    S = C.S
    x = B.dram_in("x", [T, D])
    win = B.gathered("win", 1, NCB * 128, NDC * 512)
    cos2 = B.dram_in("cos2", [T, 128])
    sinS = B.dram_in("sinS", [T, 128])
    gq = B.dram_in("gq", [1, 128])
    gk = B.dram_in("gk", [1, 128])
    ident = B.dram_in("ident", [128, 128])
    hb = B.dram_out("hb", [T, D_IN], BF16)
    with (
        nc.sbuf_tensor("xstage", [128, 2, D], F32) as xstage,
        nc.sbuf_tensor("xbf", [128, D], BF16) as xbf,
        nc.sbuf_tensor("xT", [128, NDC, T], BF16) as xT,
        nc.sbuf_tensor("identb", [128, 128], BF16) as identb,
        nc.sbuf_tensor("wsl", [128, 2, NDC * 512], BF16) as wsl,
        nc.sbuf_tensor("hbs", [128, 2, NT, 512], BF16) as hbs,
        nc.sbuf_tensor("hq", [128, NT, 1280], F32) as hq,
        nc.sbuf_tensor("hqb", [128, NT, 1280], BF16) as hqb,
        nc.sbuf_tensor("cosb", [128, NT, 128], F32) as cosb,
        nc.sbuf_tensor("sinb", [128, NT, 128], F32) as sinb,
        nc.sbuf_tensor("gqb", [128, 128], F32) as gqb,
        nc.sbuf_tensor("gkb", [128, 128], F32) as gkb,
        nc.sbuf_tensor("junk", [128, 128], F32) as junk,
        nc.sbuf_tensor("ss", [128, NT, 10], F32) as ss,
        nc.sbuf_tensor("rs", [128, NT, 10], F32) as rs,
        nc.sbuf_tensor("yy", [128, 2, 128], F32) as yy,
        nc.sbuf_tensor("t1", [128, 2, 128], F32) as t1,
        nc.sbuf_tensor("t2", [128, 2, 128], F32) as t2,
        nc.psum_tensor("ps", [128, 8, 512], F32) as ps,
    ):
        C.ps = ps
        C.psT = ps[:, :, :].bitcast(BF16)
        S.op("pool", [], extra_deps=[B.last_cc] if B.last_cc is not None else [])
        S.op("pool", lambda e: e.dma_start(out=identb[:, :], in_=ident[:, :]), writes=["ident"], dma=True)
        S.op("sp", lambda e: e.dma_start(out=cosb[:, :, :], in_=cos2.rearrange("(n p) d -> p n d", p=128)), writes=["cosb"], dma=True)
        S.op("sp", lambda e: e.dma_start(out=sinb[:, :, :], in_=sinS.rearrange("(n p) d -> p n d", p=128)), writes=["sinb"], dma=True)
        S.op("sp", lambda e: e.dma_start(out=gqb[:, :], in_=gq.partition_broadcast(128)), writes=["gqb"], dma=True)
        S.op("sp", lambda e: e.dma_start(out=gkb[:, :], in_=gk.partition_broadcast(128)), writes=["gkb"], dma=True)
        xv = x.rearrange("(n p) d -> p n d", p=128)
        for t in range(NT):
            sl = t % 2
            S.op("sp", lambda e, t=t, sl=sl: e.dma_start(out=xstage[:, sl, :], in_=xv[:, t, :]), writes=[f"xst{sl}"], dma=True)
            S.op("act", lambda e, sl=sl: e.activation(out=xbf[:, :], in_=xstage[:, sl, :], func=AF.Copy),
                 reads=[f"xst{sl}"], writes=["xbf"])
            for hh in range(2):
                bank = 6 + hh
                pst = C.psT[:, bank, :]
                fns = [(lambda e, c=hh * 8 + k, k=k, pst=pst: e.transpose(
                    out=pst[:, k * 128:(k + 1) * 128], in_=xbf[:, c * 128:(c + 1) * 128], identity=identb[:, :])) for k in range(8)]
                S.op("pe", fns, reads=["xbf", "ident"], writes=[f"ps{bank}"])
                S.op("dve", lambda e, t=t, hh=hh, pst=pst: e.tensor_copy(
                    out=xT[:, hh * 8:(hh + 1) * 8, t * 128:(t + 1) * 128], in_=pst.rearrange("p (k j) -> p k j", k=8)),
                    reads=[f"ps{bank}"], writes=[f"xT{t}"])
        hbv = hb.rearrange("(n p) d -> p n d", p=128)
        outs = []
        pctr = 0
        def proj_banks(order):
            nonlocal pctr
            for wi, n in enumerate(order):
                wslot = proj_banks.ctr % 2
                proj_banks.ctr += 1
                S.op("pool", lambda e, n=n, wslot=wslot: e.dma_start(out=wsl[:, wslot, :], in_=win[0][n * 128:(n + 1) * 128, :]),
                     reads=["win_g0"], writes=[f"wsl{wslot}"], dma=True)
                hs = n % 2
                for t in range(NT):
                    bank = pctr % 6
                    pctr += 1
                    fns = [(lambda e, c=c, t=t, bank=bank, wslot=wslot: e.matmul(
                        ps[:, bank, :], lhsT=xT[:, c, t * 128:(t + 1) * 128], rhs=wsl[:, wslot, c * 512:(c + 1) * 512],
                        start=(c == 0), stop=(c == NDC - 1))) for c in range(NDC)]
                    S.op("pe", fns, reads=[f"wsl{wslot}", f"xT{t}"], writes=[f"ps{bank}"])
                    if n < 6:
                        S.op("act", lambda e, t=t, bank=bank, hs=hs: e.activation(out=hbs[:, hs, t, :], in_=ps[:, bank, :], func=AF.Copy),
                             reads=[f"ps{bank}"], writes=[f"hbs{hs}_{t}"])
                    elif n < 8:
                        S.op("act", lambda e, t=t, bank=bank, n=n: e.activation(
                            out=hq[:, t, (n - 6) * 512:(n - 5) * 512], in_=ps[:, bank, :], func=AF.Copy),
                            reads=[f"ps{bank}"], writes=[f"hq{t}_{n}"])
                    else:
                        S.op("act", lambda e, t=t, bank=bank: e.activation(out=hq[:, t, 1024:1280], in_=ps[:, bank, 0:256], func=AF.Copy),
                             reads=[f"ps{bank}"], writes=[f"hq{t}_8"])
                        S.op("act", lambda e, t=t, bank=bank, hs=hs: e.activation(out=hbs[:, hs, t, 0:256], in_=ps[:, bank, 256:512], func=AF.Copy),
                             reads=[f"ps{bank}"], writes=[f"hbs{hs}_{t}"])
                if n < 6:
                    outs.append(S.op("sp", lambda e, n=n, hs=hs: e.dma_start(out=hbv[:, :, n * 512:(n + 1) * 512], in_=hbs[:, hs, :, :]),
                                     reads=[f"hbs{hs}_{t}" for t in range(NT)], dma=True))
                elif n == 8:
                    outs.append(S.op("sp", lambda e, hs=hs: e.dma_start(out=hbv[:, :, 4352:4608], in_=hbs[:, hs, :, 0:256]),
                                     reads=[f"hbs{hs}_{t}" for t in range(NT)], dma=True))
        proj_banks.ctr = 0
        proj_banks([6, 7, 8])
        S.op("dve", lambda e: e.memset(ss[:, :, :], 0.0), writes=[f"ss{t}_{hd}" for t in range(NT) for hd in range(10)])
        for t in range(NT):
            hk = [f"hq{t}_{n}" for n in (6, 7, 8)]
            for hd in range(10):
                S.op("act", lambda e, t=t, hd=hd: e.activation(out=junk[:, :], in_=hq[:, t, hd * 128:(hd + 1) * 128], func=AF.Square,
                                                                accum_out=ss[:, t, hd:hd + 1]),
                     reads=hk, writes=["junk", f"ss{t}_{hd}"])
            ssk = [f"ss{t}_{hd}" for hd in range(10)]
            S.op("act", lambda e, t=t: e.activation(out=rs[:, t, :], in_=ss[:, t, :], func=AF.Ln, bias=float(RMS_EPS), scale=1.0 / 128),
                 reads=ssk, writes=[f"rs{t}"])
            S.op("act", lambda e, t=t: e.activation(out=rs[:, t, :], in_=rs[:, t, :], func=AF.Exp, scale=-0.5),
                 reads=[f"rs{t}"], writes=[f"rs{t}"])
            for hd in range(10):
                s2 = hd % 2
                gb = gqb if hd < 8 else gkb
                xh = hq[:, t, hd * 128:(hd + 1) * 128]
                S.op("dve", lambda e, t=t, hd=hd, s2=s2, gb=gb, xh=xh: e.scalar_tensor_tensor(
                    out=yy[:, s2, :], in0=xh, scalar=rs[:, t, hd:hd + 1], in1=gb[:, :], op0=ALU.mult, op1=ALU.mult),
                    reads=hk + [f"rs{t}", "gqb", "gkb"], writes=[f"yy{s2}"])
                S.op("dve", lambda e, t=t, s2=s2: e.tensor_tensor(out=t1[:, s2, :], in0=yy[:, s2, :], in1=cosb[:, t, :], op=ALU.mult),
                     reads=[f"yy{s2}", "cosb"], writes=[f"t1{s2}"])
                yv = yy[:, s2, :].rearrange("p (a b c) -> p a b c", a=2, b=2)
                tv = t2[:, s2, :].rearrange("p (a b c) -> p a b c", a=2, b=2)
                sv = sinb[:, t, :].rearrange("p (a b c) -> p a b c", a=2, b=2)
                S.op("dve", lambda e, yv=yv, tv=tv, sv=sv: e.tensor_tensor(out=tv[:, :, 0, :], in0=yv[:, :, 1, :], in1=sv[:, :, 0, :], op=ALU.mult),
                     reads=[f"yy{s2}", "sinb"], writes=[f"t2a{s2}"])
                S.op("dve", lambda e, yv=yv, tv=tv, sv=sv: e.tensor_tensor(out=tv[:, :, 1, :], in0=yv[:, :, 0, :], in1=sv[:, :, 1, :], op=ALU.mult),
                     reads=[f"yy{s2}", "sinb"], writes=[f"t2b{s2}"])
                S.op("dve", lambda e, t=t, hd=hd, s2=s2: e.tensor_tensor(out=hqb[:, t, hd * 128:(hd + 1) * 128], in0=t1[:, s2, :], in1=t2[:, s2, :], op=ALU.add),
                     reads=[f"t1{s2}", f"t2a{s2}", f"t2b{s2}"], writes=[f"hqb{t}"])
        proj_banks([0, 1, 2, 3, 4, 5])
        outs.append(S.op("sp", lambda e: e.dma_start(out=hbv[:, :, 3072:4352], in_=hqb[:, :, :]),
                         reads=[f"hqb{t}" for t in range(NT)], dma=True))
        S.op("sp", [], extra_deps=outs)
        _finish(nc, C)
    return nc


def lay_win(w):
    w4 = w.reshape(NDC, 128, NCB, 512)
    return np.ascontiguousarray(w4.transpose(2, 1, 0, 3)).reshape(NCB * 128, NDC * 512)


def rope_tables(core):
    t = np.arange(core * T, (core + 1) * T)
    row = (t // 64).astype(np.float32)
    col = (t % 64).astype(np.float32)
    nfreq = 32
    inv = (1.0 / (10000.0 ** (np.arange(nfreq, dtype=np.float32) / nfreq))).astype(np.float32)
    ar = row[:, None] * inv[None, :]
    ac = col[:, None] * inv[None, :]
    cos2 = np.concatenate([np.cos(ar), np.cos(ar), np.cos(ac), np.cos(ac)], 1).astype(np.float32)
    sinS = np.concatenate([-np.sin(ar), np.sin(ar), -np.sin(ac), np.sin(ac)], 1).astype(np.float32)
    return cos2, sinS


def run_proj(xs, w_in, gq, gk):
    nc = _prog("proj", build_proj_prog)
    wl = lay_win(w_in).reshape(1, NCORES, NCB * 128 // NCORES, NDC * 512)
    ident = np.eye(128, dtype=np.float32)
    in_maps = []
    for c in range(NCORES):
        cos2, sinS = rope_tables(c)
        in_maps.append(wput({}, "win", wl, c) | {"x": xs[c], "cos2": cos2, "sinS": sinS,
                        "gq": np.ascontiguousarray(gq.reshape(1, 128)), "gk": np.ascontiguousarray(gk.reshape(1, 128)), "ident": ident})
    res = run_bass_kernel_spmd(nc, in_maps, core_ids=list(range(NCORES)))
    return [r["hb"] for r in res.results]


NEG = -30000.0
NAW = 1536


def build_attn_prog():
    B = Builder()
    nc, C = B.nc, B.C
    S = C.S
    x = B.dram_in("x", [T, D])
    wout = B.gathered("wout", 1, 4 * 128, NDC * 512)
    qnaT = B.dram_in("qnaT", [128, 8 * T], BF16)
    knaT = B.dram_in("knaT", [128, 8 * NAW], BF16)
    vna = B.dram_in("vna", [128, 12 * 1024], BF16)
    qgT = B.dram_in("qgT", [128, 8 * T], BF16)
    kgT = B.dram_in("kgT", [128, 2 * 8192], BF16)
    vg = B.dram_in("vg", [128, 64 * 256], BF16)
    lib = B.dram_in("lib", [8, 128, 1408])
    mK = B.dram_in("mK", [2, 128], BF16)
    mQ = B.dram_in("mQ", [2, 16 * 512], BF16)
    gnT = B.dram_in("gnT", [128, 16])
    lg = B.dram_in("ln_g", [1, D])
    lb = B.dram_in("ln_b", [1, D])
    y = B.dram_out("y", [T, D])
    import contextlib
    with contextlib.ExitStack() as _st:
        kv = _st.enter_context(nc.sbuf_tensor("kv", [128, 2 * 16384], BF16))
        qT = _st.enter_context(nc.sbuf_tensor("qT", [128, 8 * T], BF16))
        oT = _st.enter_context(nc.sbuf_tensor("oT", [128, 8, T], F32))
        catT = _st.enter_context(nc.sbuf_tensor("catT", [128, NDC, T], BF16))
        libs = _st.enter_context(nc.sbuf_tensor("libs", [128, 2, 1408], F32))
        tmp = _st.enter_context(nc.sbuf_tensor("tmp", [128, 2, 512], F32))
        pT = _st.enter_context(nc.sbuf_tensor("pT", [128, 3, 512], BF16))
        sq = _st.enter_context(nc.sbuf_tensor("sq", [128, 2, 512], BF16))
        rinv = _st.enter_context(nc.sbuf_tensor("rinv", [128, 512], F32))
        rstdb = _st.enter_context(nc.sbuf_tensor("rstdb", [128, 2, 512], F32))
        ones = _st.enter_context(nc.sbuf_tensor("ones", [128, 128], BF16))
        mKs = _st.enter_context(nc.sbuf_tensor("mKs", [2, 128], BF16))
        mQs = _st.enter_context(nc.sbuf_tensor("mQs", [2, 16 * 512], BF16))
        gns = _st.enter_context(nc.sbuf_tensor("gns", [128, 16], F32))
        lng = _st.enter_context(nc.sbuf_tensor("lng", [128, D], F32))
        lnb = _st.enter_context(nc.sbuf_tensor("lnb", [128, D], F32))
        bnst = _st.enter_context(nc.sbuf_tensor("bnst", [128, NT, 24], F32))
        mv = _st.enter_context(nc.sbuf_tensor("mv", [128, NT, 2], F32))
        rstd = _st.enter_context(nc.sbuf_tensor("rstd", [128, NT], F32))
        nmr = _st.enter_context(nc.sbuf_tensor("nmr", [128, NT], F32))
        ps = _st.enter_context(nc.psum_tensor("ps", [128, 8, 512], F32))
        C.ps = ps
        wos = oT[:, :, :].rearrange("p h t -> p (h t)").bitcast(BF16).rearrange("p (s n) -> p s n", s=2)
        KT = kv[:, 0:16384]
        V = kv[:, 16384:32768]
        C.xres = kv[:, :].bitcast(F32).rearrange("p (n d) -> p n d", n=NT)
        C.lng, C.lnb, C.bnst, C.mv, C.rstd, C.nmr = lng, lnb, bnst, mv, rstd, nmr
        S.op("pool", [], extra_deps=[B.last_cc] if B.last_cc is not None else [])
        S.op("pool", lambda e: e.memset(ones[:, :], 1.0), writes=["ones"])
        S.op("sp", lambda e: e.dma_start(out=mKs[:, :], in_=mK[:, :]), writes=["mK"], dma=True)
        S.op("sp", lambda e: e.dma_start(out=mQs[:, :], in_=mQ[:, :]), writes=["mQ"], dma=True)
        S.op("sp", lambda e: e.dma_start(out=gns[:, :], in_=gnT[:, :]), writes=["gns"], dma=True)
        S.op("sp", lambda e: e.dma_start(out=lng[:, :], in_=lg.partition_broadcast(128)), writes=["lng"], dma=True)
        S.op("sp", lambda e: e.dma_start(out=lnb[:, :], in_=lb.partition_broadcast(128)), writes=["lnb"], dma=True)

        state = {"sb": 0, "pt": 0, "tm": 0, "ob": 0}
        scale = 128.0 ** -0.5

        def attn_block(q_ap, chunks, out_ap, out_key):
            ob = state["ob"] % 2
            state["ob"] += 1
            bo, br = 4 + ob * 2, 5 + ob * 2
            n = len(chunks)
            pend = []

            def issue_s(j):
                kT_ap, v_ap, bias_ap, mq_ap, rk = chunks[j]
                sb = state["sb"] % 3
                state["sb"] += 1
                fns = [lambda e, sb=sb, kT_ap=kT_ap: e.matmul(ps[:, sb, :], lhsT=kT_ap, rhs=q_ap, start=True, stop=(mq_ap is None))]
                if mq_ap is not None:
                    fns.append(lambda e, sb=sb, mq_ap=mq_ap: e.matmul(ps[:, sb, :], lhsT=mKs[:, :], rhs=mq_ap, start=False, stop=True))
                S.op("pe", fns, reads=rk + ["qT", "mK", "mQ"], writes=[f"ps{sb}"])
                pt = state["pt"] % 3
                state["pt"] += 1
                if bias_ap is not None:
                    tm = state["tm"] % 2
                    state["tm"] += 1
                    S.op("dve", lambda e, sb=sb, tm=tm, bias_ap=bias_ap: e.scalar_tensor_tensor(
                        out=tmp[:, tm, :], in0=ps[:, sb, :], scalar=float(scale), in1=bias_ap, op0=ALU.mult, op1=ALU.add),
                        reads=[f"ps{sb}", "lib"], writes=[f"tmp{tm}"])
                    S.op("act", lambda e, tm=tm, pt=pt: e.activation(out=pT[:, pt, :], in_=tmp[:, tm, :], func=AF.Exp),
                         reads=[f"tmp{tm}"], writes=[f"pT{pt}"])
                else:
                    S.op("act", lambda e, sb=sb, pt=pt: e.activation(out=pT[:, pt, :], in_=ps[:, sb, :], func=AF.Exp, scale=float(scale)),
                         reads=[f"ps{sb}"], writes=[f"pT{pt}"])
                return pt

            def issue_pv(j, pt):
                kT_ap, v_ap, bias_ap, mq_ap, rk = chunks[j]
                S.op("pe", [lambda e, pt=pt, v_ap=v_ap: e.matmul(ps[:, bo, :], lhsT=v_ap, rhs=pT[:, pt, :], start=(j == 0), stop=(j == n - 1)),
                            lambda e, pt=pt: e.matmul(ps[:, br, :], lhsT=ones[:, :], rhs=pT[:, pt, :], start=(j == 0), stop=(j == n - 1))],
                     reads=rk + [f"pT{pt}", "ones"], writes=[f"ps{bo}", f"ps{br}"])

            pts = {}
            pts[0] = issue_s(0)
            if n > 1:
                pts[1] = issue_s(1)
            for j in range(n):
                if j + 2 < n:
                    pts[j + 2] = issue_s(j + 2)
                issue_pv(j, pts[j])
            S.op("dve", lambda e: e.reciprocal(out=rinv[:, :], in_=ps[:, br, :]), reads=[f"ps{br}"], writes=["rinv"])
            S.op("dve", lambda e: e.tensor_tensor(out=out_ap, in0=ps[:, bo, :], in1=rinv[:, :], op=ALU.mult),
                 reads=[f"ps{bo}", "rinv"], writes=[out_key])

        def group_norm(grp):
            for qb in range(2):
                for h in range(8):
                    s2 = h % 2
                    S.op("act", lambda e, h=h, qb=qb, s2=s2: e.activation(out=sq[:, s2, :], in_=oT[:, h, qb * 512:(qb + 1) * 512], func=AF.Square),
                         reads=[f"oT{h}_{qb}"], writes=[f"sq{s2}"])
                    S.op("pe", lambda e, h=h, s2=s2: e.matmul(ps[:, 3, :], lhsT=ones[:, :], rhs=sq[:, s2, :], start=(h == 0), stop=(h == 7)),
                         reads=[f"sq{s2}", "ones"], writes=["ps3"])
                S.op("act", lambda e, qb=qb: e.activation(out=rstdb[:, qb, :], in_=ps[:, 3, :], func=AF.Ln, bias=float(RMS_EPS), scale=1.0 / 1024),
                     reads=["ps3"], writes=[f"rstdb{qb}"])
                S.op("act", lambda e, qb=qb: e.activation(out=rstdb[:, qb, :], in_=rstdb[:, qb, :], func=AF.Exp, scale=-0.5),
                     reads=[f"rstdb{qb}"], writes=[f"rstdb{qb}"])
                for h in range(8):
                    c = grp * 8 + h
                    S.op("dve", lambda e, h=h, qb=qb, c=c: e.scalar_tensor_tensor(
                        out=catT[:, c, qb * 512:(qb + 1) * 512], in0=oT[:, h, qb * 512:(qb + 1) * 512], scalar=gns[:, c:c + 1],
                        in1=rstdb[:, qb, :], op0=ALU.mult, op1=ALU.mult),
                        reads=[f"oT{h}_{qb}", f"rstdb{qb}", "gns"], writes=[f"catT{c}_{qb}"])

        S.op("sp", lambda e: e.dma_start(out=qT[:, :], in_=qnaT[:, :]), writes=["qT"], dma=True)
        for h in range(8):
            S.op("sp", lambda e, h=h: e.dma_start(out=KT[:, h * NAW:(h + 1) * NAW], in_=knaT[:, h * NAW:(h + 1) * NAW]),
                 writes=[f"KT{h}"], dma=True)
        for tt in range(12):
            S.op("sp", lambda e, tt=tt: e.dma_start(out=V[:, tt * 1024:(tt + 1) * 1024], in_=vna[:, tt * 1024:(tt + 1) * 1024]),
                 writes=[f"V{tt}"], dma=True)
        for h in range(8):
            ls = h % 2
            S.op("sp", lambda e, h=h, ls=ls: e.dma_start(out=libs[:, ls, :], in_=lib[h, :, :]), writes=["lib"], dma=True)
            for qb in range(2):
                chunks = []
                for j in range(8):
                    tok0 = (qb * 8 + 2 * j) * 64
                    tt = tok0 // 128
                    chunks.append((KT[:, h * NAW + tok0:h * NAW + tok0 + 128],
                                   V[:, tt * 1024 + h * 128:tt * 1024 + (h + 1) * 128],
                                   libs[:, ls, (14 - 2 * j) * 64:(22 - 2 * j) * 64],
                                   mQs[:, (qb * 8 + j) * 512:(qb * 8 + j + 1) * 512],
                                   [f"KT{h}", f"V{tt}"]))
                attn_block(qT[:, h * T + qb * 512:h * T + (qb + 1) * 512], chunks, oT[:, h, qb * 512:(qb + 1) * 512], f"oT{h}_{qb}")
        group_norm(0)

        S.op("sp", lambda e: e.dma_start(out=qT[:, :], in_=qgT[:, :]), writes=["qT"], dma=True)
        ktk = [f"KT{h}" for h in range(8)]
        vtk = [f"V{tt}" for tt in range(12)]
        kgk = []
        for kh in range(2):
            for part in range(4):
                kgk.append(f"KG{kh}_{part}")
                S.op("sp", lambda e, kh=kh, part=part: e.dma_start(
                    out=KT[:, kh * 8192 + part * 2048:kh * 8192 + (part + 1) * 2048],
                    in_=kgT[:, kh * 8192 + part * 2048:kh * 8192 + (part + 1) * 2048]),
                    writes=[kgk[-1]] + ktk, dma=True)
        S.op("sp", lambda e: e.dma_start(out=V[:, 0:8192], in_=vg[:, 0:8192]), writes=["VG0"] + vtk, dma=True)
        S.op("sp", lambda e: e.dma_start(out=V[:, 8192:16384], in_=vg[:, 8192:16384]), writes=["VG1"] + vtk, dma=True)
        gkeys = kgk + ["VG0", "VG1"]
        for h in range(8):
            kh = h // 4
            for qb in range(2):
                chunks = []
                for j in range(64):
                    chunks.append((KT[:, kh * 8192 + j * 128:kh * 8192 + (j + 1) * 128],
                                   V[:, j * 256 + kh * 128:j * 256 + (kh + 1) * 128], None, None, gkeys))
                attn_block(qT[:, h * T + qb * 512:h * T + (qb + 1) * 512], chunks, oT[:, h, qb * 512:(qb + 1) * 512], f"oT{h}_{qb}")
        group_norm(1)

        allk = ktk + vtk + gkeys
        xv = x.rearrange("(n p) d -> p n d", p=128)
        for t in range(NT):
            S.op("sp", lambda e, t=t: e.dma_start(out=C.xres[:, t, :], in_=xv[:, t, :]),
                 writes=[f"xres{t}_{b}" for b in range(4)] + (allk if t == 0 else []), dma=True,
                 extra_deps=[S.ops["pe"][-1]])
        for t in range(NT):
            S.op("pool", lambda e, t=t: e.tensor_scalar(out=C.xres[:, t, :], in0=C.xres[:, t, :], scalar1=float(ALPHA),
                                                         scalar2=None, op0=ALU.mult),
                 reads=[f"xres{t}_{b}" for b in range(4)], writes=[f"xres{t}_{b}" for b in range(4)])
        pc = 0
        for b in range(4):
            ws = b % 2
            S.op("pool", lambda e, b=b, ws=ws: e.dma_start(out=wos[:, ws, :], in_=wout[0][b * 128:(b + 1) * 128, :]),
                 reads=["wout_g0"], writes=[f"wos{ws}"] + [f"oT{h}_{qb}" for h in range(8) for qb in range(2)], dma=True)
            for t in range(NT):
                bank = pc % 3
                pc += 1
                fns = [(lambda e, c=c, t=t, bank=bank, ws=ws: e.matmul(
                    ps[:, bank, :], lhsT=catT[:, c, t * 128:(t + 1) * 128], rhs=wos[:, ws, c * 512:(c + 1) * 512],
                    start=(c == 0), stop=(c == NDC - 1))) for c in range(NDC)]
                S.op("pe", fns, reads=[f"wos{ws}"] + [f"catT{c}_{t // 4}" for c in range(NDC)], writes=[f"ps{bank}"])
                S.op("dve", lambda e, t=t, b=b, bank=bank: e.tensor_tensor(
                    out=C.xres[:, t, b * 512:(b + 1) * 512], in0=ps[:, bank, :], in1=C.xres[:, t, b * 512:(b + 1) * 512], op=ALU.add),
                    reads=[f"ps{bank}", f"xres{t}_{b}"], writes=[f"xres{t}_{b}"])
        emit_ln(C)
        emit_store_x(C, y)
        _finish(nc, C)
    return nc


def na_lib(rel_bias):
    c = np.arange(64)
    kc = np.arange(64)
    c0 = np.clip(c - 8, 0, 48)
    col_in = (kc[:, None] >= c0[None, :]) & (kc[:, None] < c0[None, :] + 16)
    dcol = np.clip(kc[:, None] - c[None, :], -15, 15) + 15
    lib = np.zeros((8, 2, 64, 22, 64), np.float32)
    for s in range(22):
        for u in range(2):
            dr = 10 - s + u
            if -7 <= dr <= 7:
                vals = rel_bias[:, dr + 7][:, dcol]
                lib[:, u, :, s, :] = np.where(col_in[None], vals, np.float32(NEG))
    return np.ascontiguousarray(lib.reshape(8, 128, 22 * 64))


def na_rowmask(core):
    rows = 128
    mq = np.zeros((2, 2, 8, 8, 64), np.float32)
    for qb in range(2):
        for j in range(8):
            for u in range(2):
                kr = core * 16 + qb * 8 - 4 + 2 * j + u
                for rq in range(8):
                    qr = core * 16 + qb * 8 + rq
                    r0 = min(max(qr - 4, 0), rows - 8)
                    ok = (0 <= kr < rows) and (r0 <= kr < r0 + 8)
                    mq[u, qb, j, rq, :] = 0.0 if ok else NEG
    mK = np.zeros((2, 2, 64), np.float32)
    mK[0, 0, :] = 1.0
    mK[1, 1, :] = 1.0
    return mK.reshape(2, 128).astype(ml_dtypes.bfloat16), mq.reshape(2, 16 * 512).astype(ml_dtypes.bfloat16)


def lay_wout(w):
    w4 = w.reshape(NDC, 128, 4, 512)
    return np.ascontiguousarray(w4.transpose(2, 1, 0, 3)).reshape(4 * 128, NDC * 512)


def run_attn(xs, hb, w_out, rel_bias, gn_na, gn_gqa, g, b):
    nc = _prog("attn", build_attn_prog)
    wl = lay_wout(w_out).reshape(1, NCORES, 64, NDC * 512)
    lib = na_lib(rel_bias)
    gnT = np.ascontiguousarray(np.concatenate([gn_na, gn_gqa]).reshape(16, 128).T)
    zero_pad = np.zeros((256, 1024), hb.dtype)
    kna_p = np.concatenate([zero_pad, hb[:, 1024:2048], zero_pad], 0)
    vna_p = np.concatenate([zero_pad, hb[:, 2048:3072], zero_pad], 0)
    kgT = np.ascontiguousarray(hb[:, 4096:4352].reshape(8192, 2, 128).transpose(2, 1, 0)).reshape(128, 2 * 8192)
    vgl = np.ascontiguousarray(hb[:, 4352:4608].reshape(64, 128, 256).transpose(1, 0, 2)).reshape(128, 64 * 256)
    in_maps = []
    for c in range(NCORES):
        t0 = c * T
        qna = hb[t0:t0 + T, 0:1024]
        qg = hb[t0:t0 + T, 3072:4096]
        kwin = kna_p[t0:t0 + NAW]
        vwin = vna_p[t0:t0 + NAW]
        mK, mQ = na_rowmask(c)
        in_maps.append(wput({}, "wout", wl, c) | {
            "x": xs[c],
            "qnaT": np.ascontiguousarray(qna.reshape(T, 8, 128).transpose(2, 1, 0)).reshape(128, 8 * T),
            "knaT": np.ascontiguousarray(kwin.reshape(NAW, 8, 128).transpose(2, 1, 0)).reshape(128, 8 * NAW),
            "vna": np.ascontiguousarray(vwin.reshape(12, 128, 1024).transpose(1, 0, 2)).reshape(128, 12 * 1024),
            "qgT": np.ascontiguousarray(qg.reshape(T, 8, 128).transpose(2, 1, 0)).reshape(128, 8 * T),
            "kgT": kgT, "vg": vgl, "lib": lib, "mK": mK, "mQ": mQ, "gnT": gnT,
            "ln_g": np.ascontiguousarray(g.reshape(1, D)), "ln_b": np.ascontiguousarray(b.reshape(1, D)),
        })
    res = run_bass_kernel_spmd(nc, in_maps, core_ids=list(range(NCORES)))
    return [r["y"] for r in res.results]


def kernel(x, ffn1_w_gate_up, ffn1_w_down, ln1_g, ln1_b, w_in, na_rel_bias, q_norm_g, k_norm_g, gn_na_g, gn_gqa_g,
           w_out, ln2_g, ln2_b, ffn2_w_gate_up, ffn2_w_down, ln3_g, ln3_b):
    f = lambda a: np.asarray(a, dtype=np.float32)
    xs = [np.ascontiguousarray(f(x)[0, c * T:(c + 1) * T]) for c in range(NCORES)]
    for l in range(DEPTH):
        xs = run_ffn(xs, f(ffn1_w_gate_up)[l], f(ffn1_w_down)[l], f(ln1_g)[l], f(ln1_b)[l])
        hbs = run_proj(xs, f(w_in)[l], f(q_norm_g)[l], f(k_norm_g)[l])
        hb = np.concatenate(hbs, 0)
        xs = run_attn(xs, hb, f(w_out)[l], f(na_rel_bias)[l], f(gn_na_g)[l], f(gn_gqa_g)[l], f(ln2_g)[l], f(ln2_b)[l])
        xs = run_ffn(xs, f(ffn2_w_gate_up)[l], f(ffn2_w_down)[l], f(ln3_g)[l], f(ln3_b)[l])
    return np.concatenate(xs, 0)[None].astype(np.float32)
```
